# Optimizing a Trainium2 kernel written in Bass

```python
import math
import jax, jax.numpy as jnp
from jax import lax
import numpy as np

D_MODEL = 1024
BATCH = 8
SEQ = 2048
DEPTH = 2

GRID_W = 64
CTX_LEN = 256
HEAD_DIM = 64
N_GROUPS = 4
GROUP_W = D_MODEL // N_GROUPS
A_HEADS = GROUP_W // HEAD_DIM
A_KV_HEADS = A_HEADS // 2
A_WINDOW = 128
BLK = 128
CONV_CH = GROUP_W
CONV_K = 31
C_HEADS = GROUP_W // HEAD_DIM
C_QK_DIM = HEAD_DIM // 2
C_V_DIM = HEAD_DIM
D_HEADS = GROUP_W // HEAD_DIM
NA_KH = 8
NA_KW = 16
FFN_HIDDEN = -(-8 * D_MODEL // (3 * 256)) * 256

PROJ_SIZES = (A_HEADS * HEAD_DIM, A_KV_HEADS * HEAD_DIM, A_KV_HEADS * HEAD_DIM,
              2 * CONV_CH,
              C_HEADS * 2 * C_QK_DIM, C_HEADS * 2 * C_QK_DIM, C_HEADS * C_V_DIM,
              D_HEADS * HEAD_DIM, D_HEADS * HEAD_DIM, D_HEADS * HEAD_DIM)
IN_WIDTH = sum(PROJ_SIZES)
SPLIT_IDX = tuple(int(v) for v in np.cumsum(PROJ_SIZES)[:-1])
MIX_WIDTH = A_HEADS * HEAD_DIM + CONV_CH + C_HEADS * C_V_DIM + D_HEADS * HEAD_DIM
ROPE_BASE = 10000.0
EPS = 1e-6
NEG_INF = -1e30

kernel_name = "hybrid_parallel_groups_dit_block"


def rmsnorm(x, g):
    xf = x.astype(jnp.float32)
    r = lax.rsqrt(jnp.mean(xf * xf, axis=-1, keepdims=True) + EPS)
    return (xf * r).astype(x.dtype) * g


def layernorm(x, g, b):
    xf = x.astype(jnp.float32)
    mu = jnp.mean(xf, axis=-1, keepdims=True)
    var = jnp.mean(jnp.square(xf - mu), axis=-1, keepdims=True)
    return ((xf - mu) * lax.rsqrt(var + EPS)).astype(x.dtype) * g + b


def modulate(h, shift, scale):
    return h * (1 + scale) + shift


def axial_rope(n_tok, dim):
    t = jnp.arange(n_tok)
    row = (t // GRID_W).astype(jnp.float32)
    col = (t % GRID_W).astype(jnp.float32)
    nf = dim // 4
    inv = ROPE_BASE ** (-jnp.arange(nf, dtype=jnp.float32) / nf)
    ang = jnp.concatenate([row[:, None] * inv, col[:, None] * inv], axis=-1)
    return jnp.cos(ang), jnp.sin(ang)


def apply_rope(x, cos, sin):
    half = x.shape[-1] // 2
    shape = (1, x.shape[1]) + (1,) * (x.ndim - 3) + (half,)
    cs = cos.reshape(shape).astype(x.dtype)
    sn = sin.reshape(shape).astype(x.dtype)
    x1, x2 = x[..., :half], x[..., half:]
    return jnp.concatenate([x1 * cs - x2 * sn, x1 * sn + x2 * cs], axis=-1)


def project(h, w_in):
    b, n = h.shape[:2]
    qa, ka, va, ub, qc, kc, vc, qd, kd, vd = jnp.split(h @ w_in, SPLIT_IDX, axis=-1)
    return (qa.reshape(b, n, A_HEADS, HEAD_DIM), ka.reshape(b, n, A_KV_HEADS, HEAD_DIM),
            va.reshape(b, n, A_KV_HEADS, HEAD_DIM), ub,
            qc.reshape(b, n, C_HEADS, 2, C_QK_DIM), kc.reshape(b, n, C_HEADS, 2, C_QK_DIM),
            vc.reshape(b, n, C_HEADS, C_V_DIM),
            qd.reshape(b, n, D_HEADS, HEAD_DIM), kd.reshape(b, n, D_HEADS, HEAD_DIM),
            vd.reshape(b, n, D_HEADS, HEAD_DIM))


def ctx_attn(q, k, v, sink=None):
    b, l, h, d = q.shape
    g = k.shape[2]
    r = h // g
    qg = q.reshape(b, l, g, r, d)
    s = jnp.einsum('blgrd,bmgd->bgrlm', qg, k).astype(jnp.float32) * (d ** -0.5)
    if sink is not None:
        s_sink = jnp.broadcast_to(sink.astype(jnp.float32).reshape(1, g, r, 1, 1), s.shape[:-1] + (1,))
        s = jnp.concatenate([s, s_sink], axis=-1)
    p = jax.nn.softmax(s, axis=-1)[..., :l].astype(v.dtype)
    return jnp.einsum('bgrlm,bmgd->blgrd', p, v).reshape(b, l, h * d)


def window_gqa(q, k, v, kc, vc, sink):
    b, s_len, h, d = q.shape
    g = k.shape[2]
    r = h // g
    nb = s_len // BLK
    qb = q.reshape(b, nb, BLK, g, r, d)

    def band(t):
        tp = jnp.pad(t, ((0, 0), (BLK, BLK), (0, 0), (0, 0))).reshape(b, nb + 2, BLK, g, d)
        return jnp.concatenate([tp[:, :-2], tp[:, 1:-1], tp[:, 2:]], axis=2)

    kb, vb = band(k), band(v)
    scale = d ** -0.5
    s_loc = jnp.einsum('bnqgrd,bnkgd->bgrnqk', qb, kb).astype(jnp.float32) * scale
    s_ctx = jnp.einsum('bnqgrd,blgd->bgrnql', qb, kc).astype(jnp.float32) * scale
    blocks = jnp.arange(nb)[:, None, None] * BLK
    qpos = blocks + jnp.arange(BLK)[None, :, None]
    kpos = blocks + jnp.arange(3 * BLK)[None, None, :] - BLK
    valid = (jnp.abs(qpos - kpos) <= A_WINDOW) & (kpos >= 0) & (kpos < s_len)
    s_loc = jnp.where(valid, s_loc, NEG_INF)
    s_sink = jnp.broadcast_to(sink.astype(jnp.float32).reshape(1, g, r, 1, 1, 1), s_ctx.shape[:-1] + (1,))
    p = jax.nn.softmax(jnp.concatenate([s_loc, s_ctx, s_sink], axis=-1), axis=-1).astype(v.dtype)
    nk = 3 * BLK
    l = kc.shape[1]
    out = (jnp.einsum('bgrnqk,bnkgd->bnqgrd', p[..., :nk], vb)
           + jnp.einsum('bgrnql,blgd->bnqgrd', p[..., nk:nk + l], vc))
    return out.reshape(b, s_len, h * d)


def conformer_conv(u, w_dw, b_dw, ln_g, ln_b):
    a, gate = jnp.split(u, 2, axis=-1)
    h = a * jax.nn.sigmoid(gate)
    h = lax.conv_general_dilated(h, w_dw, window_strides=(1,),
                                 padding=((CONV_K // 2, CONV_K // 2),),
                                 dimension_numbers=('NWC', 'WIO', 'NWC'),
                                 feature_group_count=CONV_CH) + b_dw
    return jax.nn.silu(layernorm(h, ln_g, ln_b))


def diff_weights(q, k, v, lam):
    s = jnp.einsum('bqhmd,bkhmd->bhmqk', q, k).astype(jnp.float32) * (C_QK_DIM ** -0.5)
    p = jax.nn.softmax(s, axis=-1)
    w = (p[:, :, 0] - lam * p[:, :, 1]).astype(v.dtype)
    return jnp.einsum('bhqk,bkhd->bqhd', w, v)


def diff_out(o, subln_g, lambda_init):
    b, n, h, dv = o.shape
    return (rmsnorm(o, subln_g) * (1.0 - lambda_init)).reshape(b, n, h * dv)


def diff_attention_latent(q, k, v, kc, vc, lam):
    b, s_len = q.shape[:2]
    nb = s_len // BLK
    keys = jnp.concatenate([k, kc], axis=1)
    vals = jnp.concatenate([v, vc], axis=1)
    qb = jnp.moveaxis(q.reshape(b, nb, BLK, C_HEADS, 2, C_QK_DIM), 1, 0)
    o = lax.map(lambda qi: diff_weights(qi, keys, vals, lam), qb)
    return jnp.moveaxis(o, 0, 1).reshape(b, s_len, C_HEADS, C_V_DIM)


def neighbourhood_attn(q, k, v, kc, vc, rpb):
    b, s_len, h, d = q.shape
    rows = s_len // GRID_W
    kh = min(NA_KH, rows)
    qg = q.reshape(b, rows, GRID_W, h, d)
    kg = k.reshape(b, rows, GRID_W, h, d)
    vg = v.reshape(b, rows, GRID_W, h, d)
    r = jnp.arange(rows)
    rs = jnp.clip(r - kh // 2, 0, rows - kh)
    rows_idx = rs[:, None] + jnp.arange(kh)[None, :]
    kw_ = kg[:, rows_idx]
    vw_ = vg[:, rows_idx]
    cq = jnp.arange(GRID_W)
    cs = jnp.clip(cq - NA_KW // 2, 0, GRID_W - NA_KW)
    col_valid = (cq[None, :] >= cs[:, None]) & (cq[None, :] < cs[:, None] + NA_KW)
    dr = rows_idx - r[:, None] + (NA_KH - 1)
    dc = jnp.clip(cq[None, :] - cq[:, None], -(NA_KW - 1), NA_KW - 1) + (NA_KW - 1)
    bias = rpb.astype(jnp.float32)[:, dr[:, None, :, None], dc[None, :, None, :]]
    scale = d ** -0.5
    s_loc = jnp.einsum('brqhd,brkwhd->bhrqkw', qg, kw_).astype(jnp.float32) * scale + bias[None]
    s_loc = jnp.where(col_valid[:, None, :], s_loc, NEG_INF).reshape(b, h, rows, GRID_W, kh * GRID_W)
    s_ctx = jnp.einsum('brqhd,blhd->bhrql', qg, kc).astype(jnp.float32) * scale
    p = jax.nn.softmax(jnp.concatenate([s_loc, s_ctx], axis=-1), axis=-1).astype(v.dtype)
    nk = kh * GRID_W
    p_loc = p[..., :nk].reshape(b, h, rows, GRID_W, kh, GRID_W)
    out = (jnp.einsum('bhrqkw,brkwhd->brqhd', p_loc, vw_)
           + jnp.einsum('bhrql,blhd->brqhd', p[..., nk:], vc))
    return out.reshape(b, s_len, h * d)


def hybrid_mixer(hx, hc, w_in, w_out, sink, conv_w, conv_b, conv_ln_g, conv_ln_b,
                 lam, subln_g, lambda_init, rpb, rope_a, rope_c, ctx_out):
    qa, ka, va, ub, qc, kc, vc, qd, kd, vd = project(hx, w_in)
    qa_c, ka_c, va_c, ub_c, qc_c, kc_c, vc_c, qd_c, kd_c, vd_c = project(hc, w_in)
    qa, ka = apply_rope(qa, *rope_a), apply_rope(ka, *rope_a)
    qc, kc = apply_rope(qc, *rope_c), apply_rope(kc, *rope_c)
    y_a = window_gqa(qa, ka, va, ka_c, va_c, sink)
    y_b = conformer_conv(ub, conv_w, conv_b, conv_ln_g, conv_ln_b)
    y_c = diff_out(diff_attention_latent(qc, kc, vc, kc_c, vc_c, lam), subln_g, lambda_init)
    y_d = neighbourhood_attn(qd, kd, vd, kd_c, vd_c, rpb)
    yx = jnp.concatenate([y_a, y_b, y_c, y_d], axis=-1) @ w_out
    if not ctx_out:
        return yx, None
    yc_a = ctx_attn(qa_c, ka_c, va_c, sink)
    yc_b = conformer_conv(ub_c, conv_w, conv_b, conv_ln_g, conv_ln_b)
    yc_c = diff_out(diff_weights(qc_c, kc_c, vc_c, lam), subln_g, lambda_init)
    yc_d = ctx_attn(qd_c, kd_c, vd_c)
    yc = jnp.concatenate([yc_a, yc_b, yc_c, yc_d], axis=-1) @ w_out
    return yx, yc


def swiglu(h, w_gate, w_up, w_down):
    return (jax.nn.silu(h @ w_gate) * (h @ w_up)) @ w_down


def setup_inputs(seed: int = 0) -> dict:
    key = jax.random.key(seed)
    ks = jax.random.split(key, 25)
    nrm = lambda k, shape, s: jax.random.normal(k, shape, jnp.float32) * s
    D, F = D_MODEL, FFN_HIDDEN
    return {
        'x': nrm(ks[0], (BATCH, SEQ, D), 1.0),
        'c': nrm(ks[1], (BATCH, D), 1.0),
        'ctx': nrm(ks[2], (BATCH, CTX_LEN, D), 1.0),
        'c_ctx': nrm(ks[3], (D,), 1.0),
        'norm1_g': 1.0 + nrm(ks[4], (DEPTH, D), 0.1),
        'norm2_g': 1.0 + nrm(ks[5], (DEPTH, D), 0.1),
        'w_ada': nrm(ks[6], (DEPTH, D, 6 * D), 0.5 * D ** -0.5),
        'b_ada': nrm(ks[7], (DEPTH, 6 * D), 0.02),
        'w_in': nrm(ks[8], (DEPTH, D, IN_WIDTH), D ** -0.5),
        'w_out': nrm(ks[9], (DEPTH, MIX_WIDTH, D), MIX_WIDTH ** -0.5),
        'attn_sink': nrm(ks[10], (DEPTH, A_HEADS), 1.0),
        'conv_w': nrm(ks[11], (DEPTH, CONV_K, 1, CONV_CH), CONV_K ** -0.5),
        'conv_b': nrm(ks[12], (DEPTH, CONV_CH), 0.02),
        'conv_ln_g': 1.0 + nrm(ks[13], (DEPTH, CONV_CH), 0.1),
        'conv_ln_b': nrm(ks[14], (DEPTH, CONV_CH), 0.02),
        'diff_lq1': nrm(ks[15], (DEPTH, C_QK_DIM), 0.1),
        'diff_lk1': nrm(ks[16], (DEPTH, C_QK_DIM), 0.1),
        'diff_lq2': nrm(ks[17], (DEPTH, C_QK_DIM), 0.1),
        'diff_lk2': nrm(ks[18], (DEPTH, C_QK_DIM), 0.1),
        'diff_subln_g': 1.0 + nrm(ks[19], (DEPTH, C_V_DIM), 0.1),
        'na_rpb': nrm(ks[20], (DEPTH, D_HEADS, 2 * NA_KH - 1, 2 * NA_KW - 1), 0.1),
        'w_gate': nrm(ks[21], (DEPTH, D, F), D ** -0.5),
        'w_up': nrm(ks[22], (DEPTH, D, F), D ** -0.5),
        'w_down': nrm(ks[23], (DEPTH, F, D), F ** -0.5),
        'final_g': 1.0 + nrm(ks[24], (D,), 0.1),
    }


def reference(x, c, ctx, c_ctx, norm1_g, norm2_g, w_ada, b_ada, w_in, w_out, attn_sink,
              conv_w, conv_b, conv_ln_g, conv_ln_b, diff_lq1, diff_lk1, diff_lq2, diff_lk2,
              diff_subln_g, na_rpb, w_gate, w_up, w_down, final_g):
    s_len = x.shape[1]
    rope_a = axial_rope(s_len, HEAD_DIM)
    rope_c = axial_rope(s_len, C_QK_DIM)
    sc = jax.nn.silu(c)
    scc = jax.nn.silu(c_ctx)
    for l in range(DEPTH):
        ctx_needed = l < DEPTH - 1
        mx = sc @ w_ada[l] + b_ada[l]
        mc = scc @ w_ada[l] + b_ada[l]
        sh1, sc1, g1, sh2, sc2, g2 = jnp.split(mx[:, None, :], 6, axis=-1)
        csh1, csc1, cg1, csh2, csc2, cg2 = jnp.split(mc, 6, axis=-1)
        lambda_init = 0.8 - 0.6 * math.exp(-0.3 * l)
        lam = (jnp.exp(jnp.sum(diff_lq1[l].astype(jnp.float32) * diff_lk1[l].astype(jnp.float32)))
               - jnp.exp(jnp.sum(diff_lq2[l].astype(jnp.float32) * diff_lk2[l].astype(jnp.float32)))
               + lambda_init)
        hx = modulate(rmsnorm(x, norm1_g[l]), sh1, sc1)
        hc = modulate(rmsnorm(ctx, norm1_g[l]), csh1, csc1)
        yx, yc = hybrid_mixer(hx, hc, w_in[l], w_out[l], attn_sink[l], conv_w[l], conv_b[l],
                              conv_ln_g[l], conv_ln_b[l], lam, diff_subln_g[l], lambda_init,
                              na_rpb[l], rope_a, rope_c, ctx_needed)
        x = x + g1 * yx
        x = x + g2 * swiglu(modulate(rmsnorm(x, norm2_g[l]), sh2, sc2), w_gate[l], w_up[l], w_down[l])
        if ctx_needed:
            ctx = ctx + cg1 * yc
            ctx = ctx + cg2 * swiglu(modulate(rmsnorm(ctx, norm2_g[l]), csh2, csc2),
                                     w_gate[l], w_up[l], w_down[l])
    return rmsnorm(x, final_g)
```

```python
import math
from contextlib import ExitStack
import numpy as np
import concourse.bass as bass
import concourse.mybir as mybir
from concourse.bass_utils import run_bass_kernel_spmd

F32 = mybir.dt.float32
BF16 = mybir.dt.bfloat16
AF = mybir.ActivationFunctionType
ALU = mybir.AluOpType
CELL = 64

D = 1024
SEQ = 2048
CTX = 256
NTOK = SEQ + CTX
NT = NTOK // 128
NL = 2
FH = 2816
EPS = 1e-6
TB = [(0, 512), (512, 512), (1024, 512), (1536, 512), (2048, 256)]
NEG = -30000.0
OPTS = {"a_stage": 99, "c_stage": 99}


class Reg:
    __slots__ = ("name", "lo", "hi")

    def __init__(self, name, lo, hi):
        self.name, self.lo, self.hi = name, int(lo), int(hi)


class TT:
    def __init__(self, fw, name, shape, dtype, space="sbuf"):
        self.fw, self.name, self.shape, self.dtype = fw, name, list(shape), dtype
        cm = fw.nc.sbuf_tensor(name, self.shape, dtype) if space == "sbuf" else fw.nc.psum_tensor(name, self.shape, dtype)
        self.h = fw.stack.enter_context(cm)
        self.free = int(np.prod(self.shape[1:]))

    def reg(self, lo=0, hi=None):
        return Reg(self.name, lo, self.free if hi is None else hi)

    def __getitem__(self, idx):
        return self.h[idx]


class AV:
    def __init__(self, ar, off, shape, dtype=BF16):
        self.ar, self.off, self.shape, self.dtype = ar, int(off), list(shape), dtype
        self.n = int(np.prod(shape))
        self.k = 2 if dtype == F32 else 1
        assert off % CELL == 0, off
        assert self.off + self.n * self.k <= ar.free, (off, shape)

    def ap(self):
        a = self.ar.h[:, self.off:self.off + self.n * self.k]
        if self.dtype == F32:
            a = a.bitcast(F32)
        if len(self.shape) > 1:
            names = " ".join("d%d" % i for i in range(len(self.shape)))
            kw = {"d%d" % i: self.shape[i] for i in range(1, len(self.shape))}
            a = a.rearrange("p (%s) -> p %s" % (names, names), **kw)
        return a

    def reg(self, lo=0, hi=None):
        hi = self.n if hi is None else hi
        return Reg(self.ar.name, self.off + lo * self.k, self.off + hi * self.k)


class FW:
    def __init__(self, nc, stack, n_dma_sems=10):
        self.nc, self.stack = nc, stack
        self.eng = {"pe": nc.tensor, "act": nc.scalar, "dve": nc.vector, "pool": nc.gpsimd, "sp": nc.sync}
        self.sem, self.cnt = {}, {}
        for e in ("pe", "act", "dve", "pool"):
            self.sem[e] = stack.enter_context(nc.semaphore("s_" + e))
            self.cnt[e] = 0
        self.dsem, self.dcnt, self.dnext = {}, {}, {}
        for q in ("sp", "pool"):
            self.dsem[q] = [stack.enter_context(nc.semaphore("d_%s%d" % (q, i))) for i in range(n_dma_sems)]
            self.dcnt[q] = [0] * n_dma_sems
            self.dnext[q] = 0
        self.waited = {}
        self.cells = {}
        self.psbank = {}
        self.nwaits = 0
        self.nops = {"pe": 0, "act": 0, "dve": 0, "pool": 0, "sp": 0}

    def _semof(self, key):
        return self.dsem[key[1]][key[2]] if isinstance(key, tuple) else self.sem[key]

    @staticmethod
    def _need(needs, ev):
        if ev is not None and needs.get(ev[0], 0) < ev[1]:
            needs[ev[0]] = ev[1]

    def _collect(self, r, w):
        needs = {}
        for g in r:
            for c in range(g.lo // CELL, (g.hi - 1) // CELL + 1):
                st = self.cells.get((g.name, c))
                if st is not None:
                    self._need(needs, st[0])
        for g in w:
            for c in range(g.lo // CELL, (g.hi - 1) // CELL + 1):
                st = self.cells.get((g.name, c))
                if st is not None:
                    self._need(needs, st[0])
                    for k, v in st[1].items():
                        self._need(needs, (k, v))
        return needs

    def _emit_waits(self, e, needs, skip_self=False):
        engine = self.eng[e]
        for k, v in needs.items():
            if skip_self and k == e:
                continue
            if self.waited.get((e, k), 0) >= v:
                continue
            engine.wait_ge(self._semof(k), v)
            self.waited[(e, k)] = v
            self.nwaits += 1

    def _stamp(self, ev, r, w):
        k, v = ev
        for g in r:
            for c in range(g.lo // CELL, (g.hi - 1) // CELL + 1):
                st = self.cells.setdefault((g.name, c), [None, {}])
                if st[1].get(k, 0) < v:
                    st[1][k] = v
        for g in w:
            for c in range(g.lo // CELL, (g.hi - 1) // CELL + 1):
                self.cells[(g.name, c)] = [ev, {}]

    def op(self, e, fn, kw, r=(), w=(), sig=True):
        needs = self._collect(r, w)
        banks = set()
        for g in list(r) + list(w):
            if g.name == "PS":
                banks.update(range(g.lo // 512, (g.hi - 1) // 512 + 1))
        for b in banks:
            for e2, v in self.psbank.get(b, {}).items():
                if e2 != e:
                    self._need(needs, (e2, v))
        self._emit_waits(e, needs, skip_self=(e == "pe"))
        ins = fn(**kw)
        ev = (e, self.cnt[e] + 1)
        if sig:
            ins.then_inc(self.sem[e], 1)
            self.cnt[e] += 1
        self._stamp(ev, r, w)
        for b in banks:
            self.psbank.setdefault(b, {})[e] = ev[1]
        self.nops[e] += 1
        return ins

    def dma(self, q, out, in_, r=(), w=(), **kw):
        needs = self._collect(r, w)
        i = self.dnext[q]
        self.dnext[q] = (i + 1) % len(self.dsem[q])
        key = ("d", q, i)
        if self.dcnt[q][i] > 0:
            self._need(needs, (key, self.dcnt[q][i]))
        self._emit_waits(q, needs)
        self.dcnt[q][i] += 16
        ins = self.eng[q].dma_start(out=out, in_=in_, **kw)
        ins.then_inc(self.dsem[q][i], 16)
        ev = (key, self.dcnt[q][i])
        self._stamp(ev, r, w)
        self.nops[q] += 1
        return ev

    def wait_event(self, e, ev):
        self._emit_waits(e, {ev[0]: ev[1]})


def lambda_init(l):
    return 0.8 - 0.6 * math.exp(-0.3 * l)


def na_ktiles(i):
    r0, r1 = 2 * i, 2 * i + 1
    rs0 = min(max(r0 - 4, 0), 24)
    rs1 = min(max(r1 - 4, 0), 24)
    return list(range(rs0 // 2, (rs1 + 7) // 2 + 1))


NA_EDGE = (0, 1, 14, 15)


def na_block_index(i, j):
    if i in NA_EDGE:
        e = NA_EDGE.index(i)
        kl = na_ktiles(i)
        return 5 + e * 4 + kl.index(j)
    return (j - i) + 2


def build(layers=(0, 1), mixers="ABCD", do_ffn=True, do_final=True, dbg=False):
    nc = bass.Bass("TRN2", target_bir_lowering=False)

    def din(name, shape):
        return nc.dram_tensor(name, list(shape), F32, kind="ExternalInput").ap()

    x_d = din("x", [SEQ, D])
    ctx_d = din("ctx", [CTX, D])
    vecs_d = din("vecs", [384, 128])
    bvec_d = din("bvec", [392])
    rope_d = din("rope", [SEQ, 96])
    maska_d = din("maska", [128, 512])
    nab_d = din("nab", [NL, 2, 128, 2 * 21 * 128])
    w_ada_d = din("w_ada", [NL, D, 6 * D])
    w_in_d = din("w_in", [NL, D, 2560])
    w_out_d = din("w_out", [NL, D, D])
    w_gate_d = din("w_gate", [NL, D, FH])
    w_up_d = din("w_up", [NL, D, FH])
    w_down_d = din("w_down", [NL, FH, D])
    out_d = nc.dram_tensor("out", [SEQ, D], F32, kind="ExternalOutput").ap()
    dbg_d = nc.dram_tensor("dbg", [128, 8 * NTOK], F32, kind="ExternalOutput").ap() if dbg else None

    with ExitStack() as stack:
        fw = FW(nc, stack)
        V, S_, PE_, PL = nc.vector, nc.scalar, nc.tensor, nc.gpsimd

        XT = TT(fw, "XT", [128, 8, NTOK], F32)
        HT = TT(fw, "HT", [128, 8, NTOK], BF16)
        VEC = TT(fw, "VEC", [128, 384], F32)
        BV = TT(fw, "BV", [128, 392], F32)
        IDF = TT(fw, "IDF", [128, 128], F32)
        IDB = TT(fw, "IDB", [128, 128], BF16)
        ONF = TT(fw, "ONF", [128, 128], F32)
        ONB = TT(fw, "ONB", [128, 128], BF16)
        ROPE = TT(fw, "ROPE", [128, 16, 96], F32)
        MASKA = TT(fw, "MASKA", [128, 512], BF16)
        MOD = TT(fw, "MOD", [128, NL, 48, 2], F32)
        GP = TT(fw, "GP", [128, NL, 2, 8, 2], F32)
        SC2 = TT(fw, "SC2", [128, 8, 2], BF16)
        SM = TT(fw, "SM", [128, 256], F32)
        AR = TT(fw, "AR", [128, 41600], BF16)
        PS = TT(fw, "PS", [128, 4096], F32, space="psum")

        ESINK = 0
        NEGLAM = 8
        EPSC = 10
        SGV = 16
        TMPS = 160

        def sm(a, b):
            return SM[:, a:b]

        def smr(a, b):
            return SM.reg(a, b)

        def psb(bank, a=0, b=512):
            return PS[:, bank * 512 + a: bank * 512 + b]

        def psr(bank, a=0, b=512):
            return PS.reg(bank * 512 + a, bank * 512 + b)

        def psb16(bank, a=0, b=1024):
            return PS[:, bank * 512: (bank + 1) * 512].bitcast(BF16)[:, a:b]

        def psr16(bank, a=0, b=1024):
            return PS.reg(bank * 512 + a // 2, bank * 512 + (b + 1) // 2)

        def xt_reg(c, t0, n):
            return XT.reg(c * NTOK + t0, c * NTOK + t0 + n)

        def ht_reg(c, t0, n):
            return HT.reg(c * NTOK + t0, c * NTOK + t0 + n)

        fw.op("pool", PL.memset, dict(ap=IDF[:, :], constant=0.0), w=[IDF.reg()])
        fw.op("pool", PL.affine_select, dict(out=IDF[:, :], in_=IDF[:, :], pattern=[[-1, 128]], compare_op=ALU.not_equal,
                                             fill=1.0, base=0, channel_multiplier=1), r=[IDF.reg()], w=[IDF.reg()])
        fw.op("dve", V.tensor_copy, dict(out=IDB[:, :], in_=IDF[:, :]), r=[IDF.reg()], w=[IDB.reg()])
        fw.op("dve", V.memset, dict(ap=ONF[:, :], constant=1.0), w=[ONF.reg()])
        fw.op("dve", V.memset, dict(ap=ONB[:, :], constant=1.0), w=[ONB.reg()])
        fw.op("dve", V.memset, dict(ap=SM[:, :], constant=0.0), w=[SM.reg()])
        fw.op("dve", V.memset, dict(ap=sm(EPSC, EPSC + 1), constant=EPS), w=[smr(EPSC, EPSC + 1)])

        fw.dma("sp", BV[:, :], bvec_d.partition_broadcast(128), w=[BV.reg()])
        fw.dma("sp", ROPE[:, :, :], rope_d.rearrange("(t p) f -> p t f", p=128), w=[ROPE.reg()])
        fw.dma("pool", MASKA[:, :], maska_d, w=[MASKA.reg()])

        for i in range(3):
            stg = AV(AR, 2048 * (i % 2), [128], F32)
            fw.dma("sp", stg.ap(), vecs_d[i * 128:(i + 1) * 128, :], w=[stg.reg()])
            fw.op("pe", PE_.transpose, dict(out=psb(i, 0, 128), in_=stg.ap(), identity=IDF[:, :]),
                  r=[stg.reg(), IDF.reg()], w=[psr(i, 0, 128)])
            fw.op("dve", V.tensor_copy, dict(out=VEC[:, i * 128:(i + 1) * 128], in_=psb(i, 0, 128)),
                  r=[psr(i, 0, 128)], w=[VEC.reg(i * 128, (i + 1) * 128)])
        fw.op("act", S_.activation, dict(out=SC2[:, :, :].rearrange("p k s -> p s k"),
                                         in_=VEC[:, 0:16].rearrange("p (s k) -> p s k", s=2), func=AF.Silu),
              r=[VEC.reg(0, 16)], w=[SC2.reg()])
        for l in range(NL):
            b0 = 196 * l
            li = lambda_init(l)
            fw.op("act", S_.activation, dict(out=sm(ESINK + 4 * l, ESINK + 4 * l + 4), in_=BV[:, b0:b0 + 4], func=AF.Exp),
                  r=[BV.reg(b0, b0 + 4)], w=[smr(ESINK + 4 * l, ESINK + 4 * l + 4)])
            for m in range(2):
                o = b0 + 4 + 64 * m
                fw.op("dve", V.tensor_tensor, dict(out=sm(TMPS, TMPS + 32), in0=BV[:, o:o + 32], in1=BV[:, o + 32:o + 64], op=ALU.mult),
                      r=[BV.reg(o, o + 64)], w=[smr(TMPS, TMPS + 32)])
                fw.op("dve", V.tensor_reduce, dict(out=sm(TMPS + 40 + m, TMPS + 41 + m), in_=sm(TMPS, TMPS + 32),
                                                   axis=mybir.AxisListType.X, op=ALU.add),
                      r=[smr(TMPS, TMPS + 32)], w=[smr(TMPS + 40 + m, TMPS + 41 + m)])
            fw.op("act", S_.activation, dict(out=sm(TMPS + 48, TMPS + 50), in_=sm(TMPS + 40, TMPS + 42), func=AF.Exp),
                  r=[smr(TMPS + 40, TMPS + 42)], w=[smr(TMPS + 48, TMPS + 50)])
            fw.op("dve", V.tensor_tensor, dict(out=sm(TMPS + 52, TMPS + 53), in0=sm(TMPS + 49, TMPS + 50), in1=sm(TMPS + 48, TMPS + 49),
                                               op=ALU.subtract),
                  r=[smr(TMPS + 48, TMPS + 50)], w=[smr(TMPS + 52, TMPS + 53)])
            fw.op("dve", V.tensor_scalar, dict(out=sm(NEGLAM + l, NEGLAM + l + 1), in0=sm(TMPS + 52, TMPS + 53), scalar1=-li, scalar2=None,
                                               op0=ALU.add),
                  r=[smr(TMPS + 52, TMPS + 53)], w=[smr(NEGLAM + l, NEGLAM + l + 1)])
            fw.op("dve", V.tensor_scalar, dict(out=sm(SGV + 64 * l, SGV + 64 * l + 64), in0=BV[:, b0 + 132:b0 + 196], scalar1=1.0 - li,
                                               scalar2=None, op0=ALU.mult),
                  r=[BV.reg(b0 + 132, b0 + 196)], w=[smr(SGV + 64 * l, SGV + 64 * l + 64)])

        def norm(gp_ap, sh_ap, blocks, out_fn, extra_r=()):
            def bufs(bi):
                nb0 = 36864 if bi % 2 == 0 else 32768
                SQ = [AV(AR, nb0 + 512 * i, [512], BF16) for i in range(2)]
                RSTD = AV(AR, nb0 + 1024, [512], F32)
                TMP = [AV(AR, nb0 + 2048 + 1024 * i, [512], F32) for i in range(2)]
                return SQ, RSTD, TMP, 7 - (bi % 2)

            def stage_a(bi):
                t0, n = TB[bi]
                SQ, RSTD, TMP, bank = bufs(bi)
                for c in range(8):
                    if c % 2 == 0:
                        fw.op("act", S_.activation, dict(out=SQ[c % 2].ap()[:, :n], in_=XT[:, c, t0:t0 + n], func=AF.Square),
                              r=[xt_reg(c, t0, n)], w=[SQ[c % 2].reg(0, n)])
                    else:
                        fw.op("dve", V.tensor_tensor, dict(out=SQ[c % 2].ap()[:, :n], in0=XT[:, c, t0:t0 + n], in1=XT[:, c, t0:t0 + n], op=ALU.mult),
                              r=[xt_reg(c, t0, n)], w=[SQ[c % 2].reg(0, n)])
                    fw.op("pe", PE_.matmul, dict(out=psb(bank, 0, n), lhsT=ONB[:, :], rhs=SQ[c % 2].ap()[:, :n], start=(c == 0), stop=(c == 7)),
                          r=[ONB.reg(), SQ[c % 2].reg(0, n)], w=[psr(bank, 0, n)])
                fw.op("act", S_.activation, dict(out=RSTD.ap()[:, :n], in_=psb(bank, 0, n), func=AF.Ln, bias=sm(EPSC, EPSC + 1), scale=1.0 / D),
                      r=[psr(bank, 0, n), smr(EPSC, EPSC + 1)], w=[RSTD.reg(0, n)])
                fw.op("act", S_.activation, dict(out=RSTD.ap()[:, :n], in_=RSTD.ap()[:, :n], func=AF.Exp, scale=-0.5), r=[RSTD.reg(0, n)], w=[RSTD.reg(0, n)])

            def stage_b(bi):
                t0, n = TB[bi]
                s = 1 if bi == 4 else 0
                SQ, RSTD, TMP, bank = bufs(bi)
                for c in range(8):
                    fw.op("dve", V.tensor_tensor, dict(out=TMP[c % 2].ap()[:, :n], in0=XT[:, c, t0:t0 + n], in1=RSTD.ap()[:, :n], op=ALU.mult),
                          r=[xt_reg(c, t0, n), RSTD.reg(0, n)], w=[TMP[c % 2].reg(0, n)])
                    o_ap, o_regs = out_fn(c, bi)
                    kw = dict(out=o_ap, in_=TMP[c % 2].ap()[:, :n], func=AF.Identity, scale=gp_ap(c, s))
                    rr = [TMP[c % 2].reg(0, n), VEC.reg()] + list(extra_r)
                    if sh_ap is not None:
                        kw["bias"] = sh_ap(c, s)
                    fw.op("act", S_.activation, kw, r=rr, w=o_regs)

            stage_a(blocks[0])
            for k, bi in enumerate(blocks):
                if k + 1 < len(blocks):
                    stage_a(blocks[k + 1])
                stage_b(bi)

        def ht_out(c, bi):
            t0, n = TB[bi]
            return HT[:, c, t0:t0 + n], [ht_reg(c, t0, n)]

        def ada_steps(l, base=0, bank=6):
            WA = [AV(AR, base + 4096 * i, [8, 512], BF16) for i in range(2)]
            wv = w_ada_d[l].rearrange("(k p) n -> p k n", p=128)

            def dma_step(g):
                def f():
                    wa = WA[g % 2]
                    fw.dma("pool", wa.ap(), wv[:, :, g * 512:(g + 1) * 512], w=[wa.reg()])
                return f

            def mm_step(g):
                def f():
                    wa = WA[g % 2]
                    for jj in range(4):
                        j = g * 4 + jj
                        for kc in range(8):
                            fw.op("pe", PE_.matmul, dict(out=psb(bank, 2 * j, 2 * j + 2), lhsT=wa.ap()[:, kc, jj * 128:(jj + 1) * 128],
                                                         rhs=SC2[:, kc, :], start=(kc == 0), stop=(kc == 7)),
                                  r=[wa.reg(kc * 512 + jj * 128, kc * 512 + (jj + 1) * 128), SC2.reg()], w=[psr(bank, 2 * j, 2 * j + 2)],
                                  sig=(kc == 7))
                return f

            def fin():
                fw.op("dve", V.tensor_tensor, dict(out=MOD[:, l, :, :], in0=psb(bank, 0, 96).rearrange("p (j s) -> p j s", s=2),
                                                   in1=VEC[:, 48 + 48 * l:96 + 48 * l].unsqueeze(2).broadcast_to([128, 48, 2]), op=ALU.add),
                      r=[psr(bank, 0, 96), VEC.reg()], w=[MOD.reg(l * 96, (l + 1) * 96)])
                for n_ in range(2):
                    scl = MOD[:, l, 8 + 24 * n_:16 + 24 * n_, :]
                    g_ap = VEC[:, 16 + 16 * n_ + 8 * l:24 + 16 * n_ + 8 * l].unsqueeze(2).broadcast_to([128, 8, 2])
                    fw.op("dve", V.tensor_scalar, dict(out=GP[:, l, n_, :, :], in0=scl, scalar1=1.0, scalar2=None, op0=ALU.add),
                          r=[MOD.reg(l * 96, (l + 1) * 96)], w=[GP.reg(l * 32 + n_ * 16, l * 32 + (n_ + 1) * 16)])
                    fw.op("dve", V.tensor_tensor, dict(out=GP[:, l, n_, :, :], in0=GP[:, l, n_, :, :], in1=g_ap, op=ALU.mult),
                          r=[GP.reg(l * 32 + n_ * 16, l * 32 + (n_ + 1) * 16), VEC.reg()], w=[GP.reg(l * 32 + n_ * 16, l * 32 + (n_ + 1) * 16)])
            return [dma_step(g) for g in range(12)], [mm_step(g) for g in range(12)], fin

        def ada(l):
            dmas, mms, fin = ada_steps(l)
            for g in range(12):
                dmas[g]()
                mms[g]()
            fin()

        WM = AV(AR, 0, [8, 768])
        WO = AV(AR, 6144, [2, 1024])
        YT = AV(AR, 8192, [2, NTOK])
        QKS = [AV(AR, 12800 + 512 * i, [512]) for i in range(2)]
        YTILE = [AV(AR, 13824 + 1024 * i, [4, 256]) for i in range(2)]
        RT = [AV(AR, 15872 + 512 * i, [256], F32) for i in range(4)]
        PPT = AV(AR, 17920, [1024], F32)
        SP0 = 19968
        QK = AV(AR, SP0, [4, NTOK])
        VB = AV(AR, SP0 + 9216, [NT, 4, 65])
        PT0 = SP0 + 9216 + 4736

        def load_wm(l, c0, ncols):
            wv = w_in_d[l].rearrange("(k p) n -> p k n", p=128)
            for h in range(2):
                fw.dma("pool", WM.ap()[:, 4 * h:4 * h + 4, 0:ncols], wv[:, 4 * h:4 * h + 4, c0:c0 + ncols],
                       w=[WM.reg(4 * h * 768, (4 * h + 4) * 768)])

        def load_wo(l, m):
            wv = w_out_d[l][m * 256:(m + 1) * 256, :].rearrange("(k p) n -> p k n", p=128)
            fw.dma("pool", WO.ap(), wv, w=[WO.reg()])

        def inproj(l, ncols, evac):
            deferred = None
            for t in range(NT):
                banks = [(t % 2) * 2, (t % 2) * 2 + 1]
                nb = (ncols + 511) // 512
                for b in range(nb):
                    a_, b_ = b * 512, min(ncols, (b + 1) * 512)
                    for kc in range(8):
                        fw.op("pe", PE_.matmul, dict(out=psb(banks[b], 0, b_ - a_), lhsT=HT[:, kc, t * 128:(t + 1) * 128],
                                                     rhs=WM.ap()[:, kc, a_:b_], start=(kc == 0), stop=(kc == 7)),
                              r=[ht_reg(kc, t * 128, 128), WM.reg(kc * 768 + a_, kc * 768 + b_)], w=[psr(banks[b], 0, b_ - a_)],
                              sig=(kc == 7))
                nj = OPTS.get("junk", 0)
                for q_ in range(nj):
                    fw.op("pe", PE_.matmul, dict(out=psb(7, 0, 512), lhsT=IDB[:, :], rhs=HT[:, q_ % 8, 0:512], start=True, stop=True),
                          r=[IDB.reg(), ht_reg(q_ % 8, 0, 512)], w=[psr(7, 0, 512)], sig=(q_ == nj - 1))
                if deferred is not None:
                    deferred()
                deferred = evac(t, banks)
            if deferred is not None:
                deferred()

        def rope(t, src_ap, src_reg, U, Fh, tab_off, dst, dst_off):
            x = src_ap.rearrange("p (u h f) -> p u h f", u=U, h=2)
            o = dst.ap()[:, dst_off:dst_off + U * 2 * Fh].rearrange("p (u h f) -> p u h f", u=U, h=2)
            cs = ROPE[:, t, tab_off:tab_off + Fh].unsqueeze(1).broadcast_to([128, U, Fh])
            sn = ROPE[:, t, tab_off + Fh:tab_off + 2 * Fh].unsqueeze(1).broadcast_to([128, U, Fh])
            n = U * Fh
            tv = [RT[i].ap()[:, :n].rearrange("p (u f) -> p u f", u=U) for i in range(4)]
            tr = [RT[i].reg(0, n) for i in range(4)]
            rr = [src_reg, ROPE.reg()]
            fw.op("dve", V.tensor_tensor, dict(out=tv[0], in0=x[:, :, 0, :], in1=cs, op=ALU.mult), r=rr, w=[tr[0]])
            fw.op("dve", V.tensor_tensor, dict(out=tv[1], in0=x[:, :, 1, :], in1=sn, op=ALU.mult), r=rr, w=[tr[1]])
            fw.op("dve", V.tensor_tensor, dict(out=tv[2], in0=x[:, :, 0, :], in1=sn, op=ALU.mult), r=rr, w=[tr[2]])
            fw.op("dve", V.tensor_tensor, dict(out=tv[3], in0=x[:, :, 1, :], in1=cs, op=ALU.mult), r=rr, w=[tr[3]])
            dreg = dst.reg(dst_off, dst_off + U * 2 * Fh)
            fw.op("pool", PL.tensor_tensor, dict(out=o[:, :, 0, :], in0=tv[0], in1=tv[1], op=ALU.subtract), r=[tr[0], tr[1]], w=[dreg])
            fw.op("pool", PL.tensor_tensor, dict(out=o[:, :, 1, :], in0=tv[2], in1=tv[3], op=ALU.add), r=[tr[2], tr[3]], w=[dreg])

        def qk_transposes(t, qs, nch, dst, dst_tok0):
            bank = 4 + (t % 2)
            for c in range(nch):
                fw.op("pe", PE_.transpose, dict(out=psb16(bank, c * 128, (c + 1) * 128), in_=qs.ap()[:, c * 128:(c + 1) * 128], identity=IDB[:, :]),
                      r=[qs.reg(c * 128, (c + 1) * 128), IDB.reg()], w=[psr16(bank, c * 128, (c + 1) * 128)], sig=(c == nch - 1))
            W_ = dst.shape[1]
            fw.op("act", S_.copy, dict(out=dst.ap()[:, 0:nch, dst_tok0:dst_tok0 + 128],
                                       in_=psb16(bank, 0, nch * 128).rearrange("p (c t) -> p c t", c=nch)),
                  r=[psr16(bank, 0, nch * 128)], w=[dst.reg(c * W_ + dst_tok0, c * W_ + dst_tok0 + 128) for c in range(nch)])

        def v_evac(t, src_ap, src_reg, H, vb=None):
            vb = VB if vb is None else vb
            fw.op("act", S_.copy, dict(out=vb.ap()[:, t, 0:H, 0:64], in_=src_ap.rearrange("p (h d) -> p h d", h=H)),
                  r=[src_reg], w=[vb.reg(t * 260, t * 260 + H * 65)])

        def vb_ones(vb=None):
            vb = VB if vb is None else vb
            fw.op("dve", V.memset, dict(ap=vb.ap()[:, :, :, 64:65], constant=1.0), w=[vb.reg()])

        def y_transposes(yt, ntile_list, bank, col0):
            for (slot, tok0) in ntile_list:
                for c in range(2):
                    fw.op("pe", PE_.transpose, dict(out=psb16(bank, col0 + c * 128, col0 + (c + 1) * 128),
                                                    in_=yt.ap()[:, slot, c * 128:(c + 1) * 128], identity=IDB[:, :]),
                          r=[yt.reg(slot * 256 + c * 128, slot * 256 + (c + 1) * 128), IDB.reg()],
                          w=[psr16(bank, col0 + c * 128, col0 + (c + 1) * 128)], sig=(c == 1))
                fw.op("act", S_.copy, dict(out=YT.ap()[:, 0:2, tok0:tok0 + 128],
                                           in_=psb16(bank, col0, col0 + 256).rearrange("p (c t) -> p c t", c=2)),
                      r=[psr16(bank, col0, col0 + 256)], w=[YT.reg(c * NTOK + tok0, c * NTOK + tok0 + 128) for c in range(2)])

        oc_rot = [0]

        def outproj(l, blocks):
            for bi in blocks:
                t0, n = TB[bi]
                s = 1 if bi == 4 else 0
                for oc in range(8):
                    bank = oc_rot[0] % 8
                    oc_rot[0] += 1
                    for kc in range(2):
                        fw.op("pe", PE_.matmul, dict(out=psb(bank, 0, n), lhsT=WO.ap()[:, kc, oc * 128:(oc + 1) * 128], rhs=YT.ap()[:, kc, t0:t0 + n],
                                                     start=(kc == 0), stop=(kc == 1)),
                              r=[WO.reg(kc * 1024 + oc * 128, kc * 1024 + (oc + 1) * 128), YT.reg(kc * NTOK + t0, kc * NTOK + t0 + n)],
                              w=[psr(bank, 0, n)], sig=(kc == 1))
                    fw.op("dve", V.scalar_tensor_tensor, dict(out=XT[:, oc, t0:t0 + n], in0=psb(bank, 0, n), scalar=MOD[:, l, 16 + oc, s:s + 1],
                                                              in1=XT[:, oc, t0:t0 + n], op0=ALU.mult, op1=ALU.add),
                          r=[psr(bank, 0, n), MOD.reg(l * 96, (l + 1) * 96), xt_reg(oc, t0, n)], w=[xt_reg(oc, t0, n)])

        def mixer_A(l, ctx_needed):
            load_wm(l, 0, 512)
            load_wo(l, 0)
            vb_ones()

            def evac(t, banks):
                b = banks[0]
                qs = QKS[t % 2]
                if t < 16:
                    rope(t, psb(b, 0, 384), psr(b, 0, 384), 6, 32, 0, qs, 0)
                else:
                    fw.op("dve", V.tensor_copy, dict(out=qs.ap()[:, 0:384], in_=psb(b, 0, 384)), r=[psr(b, 0, 384)], w=[qs.reg(0, 384)])
                v_evac(t, psb(b, 384, 512), psr(b, 384, 512), 2)
                return lambda: qk_transposes(t, qs, 3, QK, t * 128)

            inproj(l, 512, evac)
            if OPTS["a_stage"] < 2:
                return
            PT = [AV(AR, PT0 + 1280 * i, [1280]) for i in range(2)]
            qtiles = list(range(16)) + ([16, 17] if ctx_needed else [])
            units = []
            for i in qtiles:
                for g in range(2):
                    units.append((i, g, len(units)))

            def ktl_of(i):
                if i < 16:
                    return [(j, (0 if j == i - 1 else 1 if j == i + 1 else None)) for j in (i - 1, i, i + 1) if 0 <= j < 16] + [(16, None), (17, None)]
                return [(16, None), (17, None)]

            def S_part(u):
                i, g, unit = u
                ktl = ktl_of(i)
                nk = len(ktl)
                sb = (unit % 2) * 1536
                for jj, (j, mk) in enumerate(ktl):
                    o_ap = PS[:, sb + jj * 256: sb + (jj + 1) * 256]
                    o_rg = PS.reg(sb + jj * 256, sb + (jj + 1) * 256)
                    fw.op("pe", PE_.matmul, dict(out=o_ap, lhsT=QK.ap()[64 * g:64 * g + 64, 2, j * 128:(j + 1) * 128],
                                                 rhs=QK.ap()[64 * g:64 * g + 64, 0:2, i * 128:(i + 1) * 128], start=True, stop=(mk is None)),
                          r=[QK.reg(2 * NTOK + j * 128, 2 * NTOK + (j + 1) * 128), QK.reg(i * 128, (i + 1) * 128),
                             QK.reg(NTOK + i * 128, NTOK + (i + 1) * 128)], w=[o_rg], sig=(mk is None and jj == nk - 1))
                    if mk is not None:
                        fw.op("pe", PE_.matmul, dict(out=o_ap, lhsT=IDB[:, :], rhs=MASKA[:, mk * 256:(mk + 1) * 256], start=False, stop=True),
                              r=[IDB.reg(), MASKA.reg()], w=[o_rg], sig=(jj == nk - 1))

            def EXP_part(u):
                i, g, unit = u
                nk = len(ktl_of(i))
                sb = (unit % 2) * 1536
                pt = PT[unit % 2]
                fw.op("act", S_.activation, dict(out=pt.ap()[:, 0:nk * 256], in_=PS[:, sb: sb + nk * 256], func=AF.Exp, scale=0.125),
                      r=[PS.reg(sb, sb + nk * 256)], w=[pt.reg(0, nk * 256)])

            def PV_part(u):
                i, g, unit = u
                ktl = ktl_of(i)
                nk = len(ktl)
                pt = PT[unit % 2]
                abank = 6 + (unit % 2)
                for r_ in range(2):
                    for jj, (j, mk) in enumerate(ktl):
                        fw.op("pe", PE_.matmul, dict(out=psb(abank, r_ * 128, r_ * 128 + 65), lhsT=pt.ap()[:, jj * 256 + r_ * 128: jj * 256 + (r_ + 1) * 128],
                                                     rhs=VB.ap()[:, j, g, 0:65], start=(jj == 0), stop=(jj == nk - 1)),
                              r=[pt.reg(jj * 256 + r_ * 128, jj * 256 + (r_ + 1) * 128), VB.reg(j * 260 + g * 65, j * 260 + (g + 1) * 65)],
                              w=[psr(abank, r_ * 128, r_ * 128 + 65)], sig=(jj == nk - 1))

            def POST_a(u):
                i, g, unit = u
                yt = YTILE[i % 2]
                abank = 6 + (unit % 2)
                acc = psb(abank, 0, 256).rearrange("p (r d) -> p r d", r=2)
                den = sm(TMPS + 60, TMPS + 62)
                fw.op("dve", V.tensor_tensor, dict(out=den, in0=acc[:, :, 64], in1=sm(ESINK + 4 * l + 2 * g, ESINK + 4 * l + 2 * g + 2), op=ALU.add),
                      r=[psr(abank, 0, 256), smr(ESINK, ESINK + 8)], w=[smr(TMPS + 60, TMPS + 62)])
                fw.op("dve", V.reciprocal, dict(out=den, in_=den), r=[smr(TMPS + 60, TMPS + 62)], w=[smr(TMPS + 60, TMPS + 62)])
                fw.op("dve", V.tensor_tensor, dict(out=yt.ap()[:, 0, 128 * g:128 * g + 128].rearrange("p (r d) -> p r d", r=2), in0=acc[:, :, 0:64],
                                                   in1=den.unsqueeze(2).broadcast_to([128, 2, 64]), op=ALU.mult),
                      r=[psr(abank, 0, 256), smr(TMPS + 60, TMPS + 62)], w=[yt.reg(128 * g, 128 * g + 128)])

            def POST_b(u):
                i, g, unit = u
                y_transposes(YTILE[i % 2], [(0, i * 128)], 6 + (i % 2), 512)

            S_part(units[0])
            deferred_b = None
            for k, u in enumerate(units):
                EXP_part(u)
                if k + 1 < len(units):
                    S_part(units[k + 1])
                PV_part(u)
                if deferred_b is not None:
                    POST_b(deferred_b)
                    deferred_b = None
                POST_a(u)
                if u[1] == 1:
                    deferred_b = u
            if deferred_b is not None:
                POST_b(deferred_b)
            outproj(l, [0, 1, 2, 3] + ([4] if ctx_needed else []))

        def mixer_B(l, ctx_needed):
            load_wm(l, 512, 512)
            load_wo(l, 1)
            HB = AV(AR, SP0, [2, 2364])
            DG = AV(AR, SP0 + 4736, [2, 31, 128])
            b1 = SP0 + 4736 + 7936
            CV = AV(AR, b1, [2, 512], F32)
            MEAN = AV(AR, b1 + 2048, [512], F32)
            RSD = AV(AR, b1 + 3072, [512], F32)
            SQc = [AV(AR, b1 + 4096 + 1024 * i, [512], F32) for i in range(2)]
            UU = [AV(AR, b1 + 6144 + 1024 * i, [512], F32) for i in range(2)]
            for (a_, b_) in ((0, 15), (2063, 2093), (2349, 2364)):
                fw.op("dve", V.memset, dict(ap=HB.ap()[:, :, a_:b_], constant=0.0), w=[HB.reg(a_, b_), HB.reg(2364 + a_, 2364 + b_)])
            cw0 = 164 + 62 * l
            for c in range(2):
                for k in range(31):
                    col = cw0 + 2 * k + c
                    fw.op("dve", V.tensor_scalar, dict(out=DG.ap()[:, c, k, :], in0=IDF[:, :], scalar1=VEC[:, col:col + 1], scalar2=None, op0=ALU.mult),
                          r=[IDF.reg(), VEC.reg()], w=[DG.reg((c * 31 + k) * 128, (c * 31 + k + 1) * 128)])

            def evac(t, banks):
                b = banks[0]
                qs = QKS[t % 2]
                fw.op("act", S_.activation, dict(out=RT[0].ap(), in_=psb(b, 256, 512), func=AF.Exp, scale=-1.0), r=[psr(b, 256, 512)], w=[RT[0].reg()])
                fw.op("dve", V.tensor_scalar, dict(out=RT[0].ap(), in0=RT[0].ap(), scalar1=1.0, scalar2=None, op0=ALU.add), r=[RT[0].reg()], w=[RT[0].reg()])
                fw.op("dve", V.reciprocal, dict(out=RT[0].ap(), in_=RT[0].ap()), r=[RT[0].reg()], w=[RT[0].reg()])
                fw.op("dve", V.tensor_tensor, dict(out=qs.ap()[:, 0:256], in0=psb(b, 0, 256), in1=RT[0].ap(), op=ALU.mult),
                      r=[psr(b, 0, 256), RT[0].reg()], w=[qs.reg(0, 256)])
                tok0 = 15 + t * 128 if t < 16 else 2093 + (t - 16) * 128
                return lambda: qk_transposes(t, qs, 2, HB, tok0)

            inproj(l, 512, evac)
            blocks = [0, 1, 2, 3] + ([4] if ctx_needed else [])
            for bi in blocks:
                t0, n = TB[bi]
                hb0 = t0 if bi < 4 else 2078
                for c in range(2):
                    bank = c
                    for k in range(31):
                        fw.op("pe", PE_.matmul, dict(out=psb(bank, 0, n), lhsT=DG.ap()[:, c, k, :], rhs=HB.ap()[:, c, hb0 + k: hb0 + k + n],
                                                     start=(k == 0), stop=(k == 30)),
                              r=[DG.reg((c * 31 + k) * 128, (c * 31 + k + 1) * 128), HB.reg(c * 2364 + hb0 + k, c * 2364 + hb0 + k + n)],
                              w=[psr(bank, 0, n)], sig=(k == 30))
                    cb = 152 + 2 * l + c
                    fw.op("act", S_.activation, dict(out=CV.ap()[:, c, 0:n], in_=psb(bank, 0, n), func=AF.Identity, bias=VEC[:, cb:cb + 1]),
                          r=[psr(bank, 0, n), VEC.reg()], w=[CV.reg(c * 512, c * 512 + n)])
                    fw.op("act", S_.activation, dict(out=SQc[c].ap()[:, 0:n], in_=CV.ap()[:, c, 0:n], func=AF.Square),
                          r=[CV.reg(c * 512, c * 512 + n)], w=[SQc[c].reg(0, n)])
                for c in range(2):
                    fw.op("pe", PE_.matmul, dict(out=psb(2, 0, n), lhsT=ONF[:, :], rhs=CV.ap()[:, c, 0:n], start=(c == 0), stop=(c == 1)),
                          r=[ONF.reg(), CV.reg(c * 512, c * 512 + n)], w=[psr(2, 0, n)], sig=(c == 1))
                for c in range(2):
                    fw.op("pe", PE_.matmul, dict(out=psb(3, 0, n), lhsT=ONF[:, :], rhs=SQc[c].ap()[:, 0:n], start=(c == 0), stop=(c == 1)),
                          r=[ONF.reg(), SQc[c].reg(0, n)], w=[psr(3, 0, n)], sig=(c == 1))
                fw.op("dve", V.tensor_scalar, dict(out=MEAN.ap()[:, 0:n], in0=psb(2, 0, n), scalar1=1.0 / 256, scalar2=None, op0=ALU.mult),
                      r=[psr(2, 0, n)], w=[MEAN.reg(0, n)])
                fw.op("dve", V.tensor_tensor, dict(out=RSD.ap()[:, 0:n], in0=MEAN.ap()[:, 0:n], in1=MEAN.ap()[:, 0:n], op=ALU.mult),
                      r=[MEAN.reg(0, n)], w=[RSD.reg(0, n)])
                fw.op("dve", V.scalar_tensor_tensor, dict(out=RSD.ap()[:, 0:n], in0=psb(3, 0, n), scalar=1.0 / 256, in1=RSD.ap()[:, 0:n],
                                                          op0=ALU.mult, op1=ALU.subtract),
                      r=[psr(3, 0, n), RSD.reg(0, n)], w=[RSD.reg(0, n)])
                fw.op("act", S_.activation, dict(out=RSD.ap()[:, 0:n], in_=RSD.ap()[:, 0:n], func=AF.Ln, bias=sm(EPSC, EPSC + 1)),
                      r=[RSD.reg(0, n), smr(EPSC, EPSC + 1)], w=[RSD.reg(0, n)])
                fw.op("act", S_.activation, dict(out=RSD.ap()[:, 0:n], in_=RSD.ap()[:, 0:n], func=AF.Exp, scale=-0.5), r=[RSD.reg(0, n)], w=[RSD.reg(0, n)])
                for c in range(2):
                    fw.op("dve", V.tensor_tensor, dict(out=UU[c].ap()[:, 0:n], in0=CV.ap()[:, c, 0:n], in1=MEAN.ap()[:, 0:n], op=ALU.subtract),
                          r=[CV.reg(c * 512, c * 512 + n), MEAN.reg(0, n)], w=[UU[c].reg(0, n)])
                    fw.op("dve", V.tensor_tensor, dict(out=UU[c].ap()[:, 0:n], in0=UU[c].ap()[:, 0:n], in1=RSD.ap()[:, 0:n], op=ALU.mult),
                          r=[UU[c].reg(0, n), RSD.reg(0, n)], w=[UU[c].reg(0, n)])
                    lg, lb = 156 + 2 * l + c, 160 + 2 * l + c
                    fw.op("act", S_.activation, dict(out=UU[c].ap()[:, 0:n], in_=UU[c].ap()[:, 0:n], func=AF.Identity,
                                                     scale=VEC[:, lg:lg + 1], bias=VEC[:, lb:lb + 1]),
                          r=[UU[c].reg(0, n), VEC.reg()], w=[UU[c].reg(0, n)])
                    fw.op("act", S_.activation, dict(out=SQc[c].ap()[:, 0:n], in_=UU[c].ap()[:, 0:n], func=AF.Exp, scale=-1.0),
                          r=[UU[c].reg(0, n)], w=[SQc[c].reg(0, n)])
                    fw.op("dve", V.tensor_scalar, dict(out=SQc[c].ap()[:, 0:n], in0=SQc[c].ap()[:, 0:n], scalar1=1.0, scalar2=None, op0=ALU.add),
                          r=[SQc[c].reg(0, n)], w=[SQc[c].reg(0, n)])
                    fw.op("dve", V.reciprocal, dict(out=SQc[c].ap()[:, 0:n], in_=SQc[c].ap()[:, 0:n]), r=[SQc[c].reg(0, n)], w=[SQc[c].reg(0, n)])
                    fw.op("dve", V.tensor_tensor, dict(out=YT.ap()[:, c, t0:t0 + n], in0=UU[c].ap()[:, 0:n], in1=SQc[c].ap()[:, 0:n], op=ALU.mult),
                          r=[UU[c].reg(0, n), SQc[c].reg(0, n)], w=[YT.reg(c * NTOK + t0, c * NTOK + t0 + n)])
            outproj(l, blocks)

        def mixer_C(l, ctx_needed):
            load_wm(l, 1024, 768)
            load_wo(l, 2)
            QK6 = AV(AR, SP0, [6, NTOK])
            VBc = AV(AR, SP0 + 13824, [NT, 4, 65])
            PTC0 = SP0 + 13824 + 4736
            vb_ones(VBc)

            def evac(t, banks):
                b0_, b1_ = banks
                qs = QKS[t % 2]
                if t < 16:
                    rope(t, psb(b0_, 0, 512), psr(b0_, 0, 512), 16, 16, 64, qs, 0)
                else:
                    fw.op("dve", V.tensor_copy, dict(out=qs.ap()[:, 0:512], in_=psb(b0_, 0, 512)), r=[psr(b0_, 0, 512)], w=[qs.reg(0, 512)])
                v_evac(t, psb(b1_, 0, 256), psr(b1_, 0, 256), 4, VBc)

                def part_b():
                    bank = 4 + (t % 2)
                    for c in range(4):
                        fw.op("pe", PE_.transpose, dict(out=psb16(bank, c * 128, (c + 1) * 128), in_=qs.ap()[:, c * 128:(c + 1) * 128], identity=IDB[:, :]),
                              r=[qs.reg(c * 128, (c + 1) * 128), IDB.reg()], w=[psr16(bank, c * 128, (c + 1) * 128)], sig=(c == 3))
                    tk = t * 128
                    pq = psb16(bank, 0, 256).rearrange("p (c t) -> p c t", c=2)
                    pk = psb16(bank, 256, 512).rearrange("p (c t) -> p c t", c=2)
                    fw.op("act", S_.activation, dict(out=QK6.ap()[:, 0:2, tk:tk + 128], in_=pq, func=AF.Identity, scale=VEC[:, 288:289]),
                          r=[psr16(bank, 0, 256), VEC.reg()], w=[QK6.reg(c * NTOK + tk, c * NTOK + tk + 128) for c in (0, 1)])
                    fw.op("act", S_.activation, dict(out=QK6.ap()[:, 2:4, tk:tk + 128], in_=pq, func=AF.Identity, scale=VEC[:, 289:290]),
                          r=[psr16(bank, 0, 256), VEC.reg()], w=[QK6.reg(c * NTOK + tk, c * NTOK + tk + 128) for c in (2, 3)])
                    fw.op("act", S_.copy, dict(out=QK6.ap()[:, 4:6, tk:tk + 128], in_=pk),
                          r=[psr16(bank, 256, 512)], w=[QK6.reg(c * NTOK + tk, c * NTOK + tk + 128) for c in (4, 5)])
                return part_b

            inproj(l, 768, evac)
            if OPTS["c_stage"] < 2:
                return
            PT = [AV(AR, PTC0 + 1024 * i, [1024]) for i in range(3)]
            scale = 32 ** -0.5
            qblocks = [(q0, 256, list(range(18))) for q0 in range(0, SEQ, 256)]
            if ctx_needed:
                qblocks.append((2048, 256, [16, 17]))
            sctr = 0
            last_off = [None]
            pending_tr = []
            for qbi, (q0, nq, ktl) in enumerate(qblocks):
                yt = YTILE[qbi % 2]
                for h in (0, 2, 1, 3):
                    nk = len(ktl)
                    off = 64 * (h % 2)
                    kc_ = 4 + h // 2

                    npair = nk // 2

                    def s_mm(p):
                        sb2 = (sctr + p) % 2
                        c0_ = h // 2
                        for e in range(2):
                            j = ktl[2 * p + e]
                            fw.op("pe", PE_.matmul, dict(out=psb(2 * sb2 + e, 0, 512).rearrange("p (m q) -> p m q", m=2),
                                                         lhsT=QK6.ap()[off:off + 64, kc_, j * 128:(j + 1) * 128],
                                                         rhs=QK6.ap()[off:off + 64, c0_:c0_ + 3:2, q0:q0 + nq], start=True, stop=True),
                                  r=[QK6.reg(kc_ * NTOK + j * 128, kc_ * NTOK + (j + 1) * 128), QK6.reg(c0_ * NTOK + q0, c0_ * NTOK + q0 + nq),
                                     QK6.reg((c0_ + 2) * NTOK + q0, (c0_ + 2) * NTOK + q0 + nq)],
                                  w=[psr(2 * sb2 + e, 0, 512)], sig=(e == 1))

                    if last_off[0] != off and fw.cnt["pe"] > 0:
                        PE_.wait_ge(fw.sem["pe"], fw.cnt["pe"])
                    last_off[0] = off
                    s_mm(0)
                    for p in range(npair):
                        sb2 = (sctr + p) % 2
                        pt = PT[(sctr + p) % 3]
                        fw.op("act", S_.activation, dict(out=pt.ap(), in_=PS[:, 2 * sb2 * 512:(2 * sb2 + 2) * 512], func=AF.Exp, scale=scale),
                              r=[psr(2 * sb2), psr(2 * sb2 + 1)], w=[pt.reg()])
                        if p + 1 < npair:
                            s_mm(p + 1)
                        for e in range(2):
                            j = ktl[2 * p + e]
                            for m in range(2):
                                for qt in range(2):
                                    ab = 4 + 2 * m + qt
                                    o_ = e * 512 + m * 256 + qt * 128
                                    fw.op("pe", PE_.matmul, dict(out=psb(ab, 0, 65), lhsT=pt.ap()[:, o_:o_ + 128],
                                                                 rhs=VBc.ap()[:, j, h, 0:65], start=(p == 0 and e == 0), stop=(p == npair - 1 and e == 1)),
                                          r=[pt.reg(o_, o_ + 128), VBc.reg(j * 260 + h * 65, j * 260 + (h + 1) * 65)],
                                          w=[psr(ab, 0, 65)], sig=((p == npair - 1 and e == 1) or (e == 1 and m == 1 and qt == 1)))
                        if p == 1 and pending_tr:
                            pending_tr.pop(0)()
                    nk = npair
                    sctr += nk
                    if OPTS["c_stage"] < 5:
                        continue
                    accall = PS[:, 2048:4096].rearrange("p (b c) -> p b c", b=4)
                    accr = [psr(4, 0, 65), psr(5, 0, 65), psr(6, 0, 65), psr(7, 0, 65)]
                    rec = sm(TMPS + 64, TMPS + 68)
                    recr = smr(TMPS + 64, TMPS + 68)
                    fw.op("dve", V.reciprocal, dict(out=rec, in_=accall[:, :, 64]), r=accr, w=[recr])
                    fw.op("dve", V.tensor_scalar, dict(out=rec[:, 2:4], in0=rec[:, 2:4], scalar1=sm(NEGLAM + l, NEGLAM + l + 1), scalar2=None, op0=ALU.mult),
                          r=[recr, smr(NEGLAM, NEGLAM + 2)], w=[recr])
                    A_ = PPT.ap()[:, 0:128].rearrange("p (q d) -> p q d", q=2)
                    B_ = PPT.ap()[:, 256:384].rearrange("p (q d) -> p q d", q=2)
                    C_ = PPT.ap()[:, 512:640].rearrange("p (q d) -> p q d", q=2)
                    ar_, br_, cr_ = PPT.reg(0, 128), PPT.reg(256, 384), PPT.reg(512, 640)
                    fw.op("dve", V.tensor_tensor, dict(out=A_, in0=accall[:, 0:2, 0:64], in1=rec[:, 0:2].unsqueeze(2).broadcast_to([128, 2, 64]), op=ALU.mult),
                          r=accr[0:2] + [recr], w=[ar_])
                    fw.op("dve", V.tensor_tensor, dict(out=B_, in0=accall[:, 2:4, 0:64], in1=rec[:, 2:4].unsqueeze(2).broadcast_to([128, 2, 64]), op=ALU.mult),
                          r=accr[2:4] + [recr], w=[br_])
                    fw.op("dve", V.tensor_tensor, dict(out=A_, in0=A_, in1=B_, op=ALU.add), r=[ar_, br_], w=[ar_])
                    fw.op("dve", V.tensor_tensor, dict(out=C_, in0=A_, in1=A_, op=ALU.mult), r=[ar_], w=[cr_])
                    ss = sm(TMPS + 72, TMPS + 74)
                    ssr = smr(TMPS + 72, TMPS + 74)
                    fw.op("dve", V.tensor_reduce, dict(out=ss, in_=C_, axis=mybir.AxisListType.X, op=ALU.add), r=[cr_], w=[ssr])
                    fw.op("act", S_.activation, dict(out=ss, in_=ss, func=AF.Ln, bias=sm(EPSC, EPSC + 1), scale=1.0 / 64),
                          r=[ssr, smr(EPSC, EPSC + 1)], w=[ssr])
                    fw.op("act", S_.activation, dict(out=ss, in_=ss, func=AF.Exp, scale=-0.5), r=[ssr], w=[ssr])
                    fw.op("dve", V.tensor_tensor, dict(out=A_, in0=A_, in1=ss.unsqueeze(2).broadcast_to([128, 2, 64]), op=ALU.mult),
                          r=[ar_, ssr], w=[ar_])
                    fw.op("dve", V.tensor_tensor, dict(out=yt.ap()[:, 0:2, h * 64:(h + 1) * 64], in0=A_,
                                                       in1=sm(SGV + 64 * l, SGV + 64 * l + 64).unsqueeze(1).broadcast_to([128, 2, 64]), op=ALU.mult),
                          r=[ar_, smr(SGV, SGV + 128)], w=[yt.reg(qt * 256 + h * 64, qt * 256 + (h + 1) * 64) for qt in range(2)])
                if OPTS["c_stage"] < 6:
                    continue
                while pending_tr:
                    pending_tr.pop(0)()
                pending_tr.append(lambda yt=yt, q0=q0, qbi=qbi: y_transposes(yt, [(qt, q0 + qt * 128) for qt in range(2)], 3, 0))
            while pending_tr:
                pending_tr.pop(0)()
            outproj(l, [0, 1, 2, 3] + ([4] if ctx_needed else []))

        def mixer_D(l, ctx_needed):
            load_wm(l, 1792, 768)
            load_wo(l, 3)
            vb_ones()
            NB = AV(AR, PT0 + 1792, [2, 21, 128])

            def evac(t, banks):
                b0_, b1_ = banks
                qs = QKS[t % 2]
                fw.op("act", S_.activation, dict(out=qs.ap()[:, 0:256], in_=psb(b0_, 0, 256), func=AF.Copy, scale=0.125),
                      r=[psr(b0_, 0, 256)], w=[qs.reg(0, 256)])
                fw.op("dve", V.tensor_copy, dict(out=qs.ap()[:, 256:512], in_=psb(b0_, 256, 512)), r=[psr(b0_, 256, 512)], w=[qs.reg(256, 512)])
                v_evac(t, psb(b1_, 0, 256), psr(b1_, 0, 256), 4)
                return lambda: qk_transposes(t, qs, 4, QK, t * 128)

            inproj(l, 768, evac)
            PT = [AV(AR, PT0 + 896 * i, [896]) for i in range(2)]
            qtiles = list(range(16)) + ([16, 17] if ctx_needed else [])
            hus = []
            unit = 0
            for hp in range(2):
                for i in qtiles:
                    for hh in range(2):
                        hus.append((hp, i, hh, unit))
                    unit += 1

            def ktl_of(i):
                if i < 16:
                    return [(j, na_block_index(i, j)) for j in na_ktiles(i)] + [(16, None), (17, None)]
                return [(16, None), (17, None)]

            def S_part(hu):
                hp, i, hh, unit = hu
                if i == qtiles[0] and hh == 0:
                    fw.dma("pool", NB.ap(), nab_d[l, hp].rearrange("p (h b q) -> p h b q", h=2, b=21), w=[NB.reg()])
                ktl = ktl_of(i)
                nk = len(ktl)
                sb = hh * 1024
                off = 64 * hh
                nloc = sum(1 for (_, bidx) in ktl if bidx is not None)
                for jj, (j, bidx) in enumerate(ktl):
                    o_ap = PS[:, sb + jj * 128: sb + (jj + 1) * 128]
                    o_rg = PS.reg(sb + jj * 128, sb + (jj + 1) * 128)
                    fw.op("pe", PE_.matmul, dict(out=o_ap, lhsT=QK.ap()[off:off + 64, 2 + hp, j * 128:(j + 1) * 128],
                                                 rhs=QK.ap()[off:off + 64, hp, i * 128:(i + 1) * 128], start=(jj % 4 == 0), stop=True,
                                                 skip_group_check=True),
                          r=[QK.reg((2 + hp) * NTOK + j * 128, (2 + hp) * NTOK + (j + 1) * 128), QK.reg(hp * NTOK + i * 128, hp * NTOK + (i + 1) * 128)],
                          w=[o_rg], sig=(nloc == 0 and jj == nk - 1))
                if nloc > 0:
                    b0 = ktl[0][1]
                    n1 = min(nloc, 4)
                    fw.op("pe", PE_.matmul, dict(out=PS[:, sb: sb + n1 * 128], lhsT=IDB[:, :],
                                                 rhs=NB.ap()[:, hh, b0:b0 + n1, :].rearrange("p b q -> p (b q)"), start=False, stop=True,
                                                 skip_group_check=True),
                          r=[IDB.reg(), NB.reg((hh * 21 + b0) * 128, (hh * 21 + b0 + n1) * 128)], w=[PS.reg(sb, sb + n1 * 128)], sig=(nloc <= 4))
                    if nloc > 4:
                        fw.op("pe", PE_.matmul, dict(out=PS[:, sb + 512: sb + 640], lhsT=IDB[:, :], rhs=NB.ap()[:, hh, b0 + 4, :], start=False, stop=True,
                                                     skip_group_check=True),
                              r=[IDB.reg(), NB.reg((hh * 21 + b0 + 4) * 128, (hh * 21 + b0 + 5) * 128)], w=[PS.reg(sb + 512, sb + 640)])

            def EXP_part(hu):
                hp, i, hh, unit = hu
                nk = len(ktl_of(i))
                sb = hh * 1024
                pt = PT[hh]
                fw.op("act", S_.activation, dict(out=pt.ap()[:, 0:nk * 128], in_=PS[:, sb: sb + nk * 128], func=AF.Exp),
                      r=[PS.reg(sb, sb + nk * 128)], w=[pt.reg(0, nk * 128)])

            def PV_part(hu):
                hp, i, hh, unit = hu
                ktl = ktl_of(i)
                nk = len(ktl)
                h = 2 * hp + hh
                abank = 4 + (unit % 2)
                pt = PT[hh]
                for jj, (j, bidx) in enumerate(ktl):
                    fw.op("pe", PE_.matmul, dict(out=psb(abank, hh * 128, hh * 128 + 65), lhsT=pt.ap()[:, jj * 128:(jj + 1) * 128],
                                                 rhs=VB.ap()[:, j, h, 0:65], start=(jj == 0), stop=(jj == nk - 1)),
                          r=[pt.reg(jj * 128, (jj + 1) * 128), VB.reg(j * 260 + h * 65, j * 260 + (h + 1) * 65)],
                          w=[psr(abank, hh * 128, hh * 128 + 65)], sig=(jj == nk - 1))

            def POST_a(hu):
                hp, i, hh, unit = hu
                yt = YTILE[i % 2]
                abank = 4 + (unit % 2)
                acc = psb(abank, 0, 256).rearrange("p (r d) -> p r d", r=2)
                den = sm(TMPS + 80, TMPS + 82)
                denr = smr(TMPS + 80, TMPS + 82)
                fw.op("dve", V.reciprocal, dict(out=den, in_=acc[:, :, 64]), r=[psr(abank, 0, 256)], w=[denr])
                fw.op("dve", V.tensor_tensor, dict(out=yt.ap()[:, hp, 0:128].rearrange("p (r d) -> p r d", r=2), in0=acc[:, :, 0:64],
                                                   in1=den.unsqueeze(2).broadcast_to([128, 2, 64]), op=ALU.mult),
                      r=[psr(abank, 0, 256), denr], w=[yt.reg(hp * 256, hp * 256 + 128)])

            def POST_b(hu):
                hp, i, hh, unit = hu
                yt = YTILE[i % 2]
                tb_ = 6 + (unit % 2)
                fw.op("pe", PE_.transpose, dict(out=psb16(tb_, 0, 128), in_=yt.ap()[:, hp, 0:128], identity=IDB[:, :]),
                      r=[yt.reg(hp * 256, hp * 256 + 128), IDB.reg()], w=[psr16(tb_, 0, 128)])
                fw.op("act", S_.copy, dict(out=YT.ap()[:, hp, i * 128:(i + 1) * 128], in_=psb16(tb_, 0, 128)),
                      r=[psr16(tb_, 0, 128)], w=[YT.reg(hp * NTOK + i * 128, hp * NTOK + (i + 1) * 128)])

            S_part(hus[0])
            deferred_b = None
            for k, hu in enumerate(hus):
                EXP_part(hu)
                if k + 1 < len(hus):
                    S_part(hus[k + 1])
                PV_part(hu)
                if deferred_b is not None:
                    POST_b(deferred_b)
                    deferred_b = None
                if hu[2] == 1:
                    POST_a(hu)
                    deferred_b = hu
            if deferred_b is not None:
                POST_b(deferred_b)
            outproj(l, [0, 1, 2, 3] + ([4] if ctx_needed else []))

        PASSES = [(0, 4), (4, 4), (8, 4), (12, 4), (16, 3), (19, 3)]

        def ffn(l, ctx_needed, prefetch_ada=None):
            WG = [AV(AR, 4096 * i, [8, 512]) for i in range(2)]
            WU = [AV(AR, 8192 + 4096 * i, [8, 512]) for i in range(2)]
            WD = [AV(AR, 16384 + 4096 * i, [4, 1024]) for i in range(2)]
            AT = [AV(AR, 24576 + 2048 * i, [4, 512]) for i in range(2)]
            SG = [AV(AR, 28672 + 1024 * i, [512], F32) for i in range(2)]
            gv = w_gate_d[l].rearrange("(k p) n -> p k n", p=128)
            uv = w_up_d[l].rearrange("(k p) n -> p k n", p=128)
            blocks = [0, 1, 2, 3] + ([4] if ctx_needed else [])
            cnt = 0
            ada_q = []
            if prefetch_ada is not None:
                dmas, mms, fin = ada_steps(prefetch_ada, base=30720, bank=7)
                ada_q = [dmas[0]] + [(lambda g=g: (dmas[g + 1]() if g + 1 < 12 else None, mms[g]())) for g in range(12)] + [fin]

            def load_pass(pi):
                m0, nm = PASSES[pi]
                wg, wu, wd = WG[pi % 2], WU[pi % 2], WD[pi % 2]
                fw.dma("pool", wg.ap()[:, :, 0:nm * 128], gv[:, :, m0 * 128:(m0 + nm) * 128], w=[wg.reg()])
                fw.dma("pool", wu.ap()[:, :, 0:nm * 128], uv[:, :, m0 * 128:(m0 + nm) * 128], w=[wu.reg()])
                fw.dma("pool", wd.ap()[:, 0:nm, :], w_down_d[l][m0 * 128:(m0 + nm) * 128, :].rearrange("(m p) n -> p m n", p=128), w=[wd.reg()])

            load_pass(0)
            for pi, (m0, nm) in enumerate(PASSES):
                wg, wu, wd = WG[pi % 2], WU[pi % 2], WD[pi % 2]
                if pi + 1 < len(PASSES):
                    load_pass(pi + 1)
                for bi in blocks:
                    t0, n = TB[bi]
                    s = 1 if bi == 4 else 0
                    at = AT[cnt % 2]
                    for mm in range(nm):
                        gb, ub = 2 * (mm % 2), 2 * (mm % 2) + 1
                        for (bank, wt) in ((gb, wg), (ub, wu)):
                            for kc in range(8):
                                fw.op("pe", PE_.matmul, dict(out=psb(bank, 0, n), lhsT=wt.ap()[:, kc, mm * 128:(mm + 1) * 128], rhs=HT[:, kc, t0:t0 + n],
                                                             start=(kc == 0), stop=(kc == 7)),
                                      r=[wt.reg(kc * 512 + mm * 128, kc * 512 + (mm + 1) * 128), ht_reg(kc, t0, n)], w=[psr(bank, 0, n)], sig=(kc == 7))
                        sg = SG[mm % 2]
                        fw.op("act", S_.activation, dict(out=sg.ap()[:, 0:n], in_=psb(gb, 0, n), func=AF.Silu), r=[psr(gb, 0, n)], w=[sg.reg(0, n)])
                        fw.op("dve", V.tensor_tensor, dict(out=at.ap()[:, mm, 0:n], in0=psb(ub, 0, n), in1=sg.ap()[:, 0:n], op=ALU.mult),
                              r=[psr(ub, 0, n), sg.reg(0, n)], w=[at.reg(mm * 512, mm * 512 + n)])
                    for oc in range(8):
                        bank = 4 + (oc % 3)
                        for mm in range(nm):
                            fw.op("pe", PE_.matmul, dict(out=psb(bank, 0, n), lhsT=wd.ap()[:, mm, oc * 128:(oc + 1) * 128], rhs=at.ap()[:, mm, 0:n],
                                                         start=(mm == 0), stop=(mm == nm - 1)),
                                  r=[wd.reg(mm * 1024 + oc * 128, mm * 1024 + (oc + 1) * 128), at.reg(mm * 512, mm * 512 + n)], w=[psr(bank, 0, n)],
                                  sig=(mm == nm - 1))
                        fw.op("dve", V.scalar_tensor_tensor, dict(out=XT[:, oc, t0:t0 + n], in0=psb(bank, 0, n), scalar=MOD[:, l, 40 + oc, s:s + 1],
                                                                  in1=XT[:, oc, t0:t0 + n], op0=ALU.mult, op1=ALU.add),
                              r=[psr(bank, 0, n), MOD.reg(l * 96, (l + 1) * 96), xt_reg(oc, t0, n)], w=[xt_reg(oc, t0, n)])
                    cnt += 1
                    if ada_q:
                        ada_q.pop(0)()
            while ada_q:
                ada_q.pop(0)()

        def load_x():
            for t in range(NT):
                stg = AV(AR, 8192 + 2048 * (t % 2), [1024], F32)
                src = x_d[t * 128:(t + 1) * 128, :] if t < 16 else ctx_d[(t - 16) * 128:(t - 15) * 128, :]
                fw.dma("sp", stg.ap(), src, w=[stg.reg()])
                for hb in range(2):
                    bank = (t % 2) * 2 + hb
                    for cc in range(4):
                        c = hb * 4 + cc
                        fw.op("pe", PE_.transpose, dict(out=psb(bank, cc * 128, (cc + 1) * 128), in_=stg.ap()[:, c * 128:(c + 1) * 128],
                                                        identity=IDF[:, :]),
                              r=[stg.reg(c * 128, (c + 1) * 128), IDF.reg()], w=[psr(bank, cc * 128, (cc + 1) * 128)], sig=(cc == 3))
                    eng = "act" if hb == 0 else "dve"
                    fn = S_.copy if hb == 0 else V.tensor_copy
                    fw.op(eng, fn, dict(out=XT[:, hb * 4:hb * 4 + 4, t * 128:(t + 1) * 128],
                                        in_=psb(bank).rearrange("p (c t) -> p c t", c=4)),
                          r=[psr(bank)], w=[xt_reg(c_, t * 128, 128) for c_ in range(hb * 4, hb * 4 + 4)])


        mix_fns = {"A": mixer_A, "B": mixer_B, "C": mixer_C, "D": mixer_D}
        ada(layers[0])
        load_x()
        for l in layers:
            ctx_needed = l < NL - 1
            if l != layers[0] and not do_ffn:
                ada(l)
            norm(lambda c, s, l=l: GP[:, l, 0, c, s:s + 1], lambda c, s, l=l: MOD[:, l, c, s:s + 1], [0, 1, 2, 3, 4], ht_out,
                 extra_r=[GP.reg(l * 32, l * 32 + 32), MOD.reg(l * 96, (l + 1) * 96)])
            for m in mixers:
                mix_fns[m](l, ctx_needed)
            if do_ffn:
                norm(lambda c, s, l=l: GP[:, l, 1, c, s:s + 1], lambda c, s, l=l: MOD[:, l, 24 + c, s:s + 1],
                     [0, 1, 2, 3] + ([4] if ctx_needed else []), ht_out,
                     extra_r=[GP.reg(l * 32, l * 32 + 32), MOD.reg(l * 96, (l + 1) * 96)])
                ffn(l, ctx_needed, prefetch_ada=(l + 1 if (l + 1) in layers else None))

        out_events = []
        if do_final:
            FN = AV(AR, 0, [8, 512], F32)
            OS = [AV(AR, 8192 + 2048 * i, [1024], F32) for i in range(2)]

            def fn_out(c, bi):
                return FN.ap()[:, c, :], [FN.reg(c * 512, (c + 1) * 512)]

            for bi in range(4):
                norm(lambda c, s: VEC[:, 144 + c:145 + c], None, [bi], fn_out)
                for tt in range(4):
                    t = bi * 4 + tt
                    os_ = OS[t % 2]
                    for hb in range(2):
                        bank = (t % 2) * 2 + hb
                        for cc in range(4):
                            c = hb * 4 + cc
                            fw.op("pe", PE_.transpose, dict(out=psb(bank, cc * 128, (cc + 1) * 128), in_=FN.ap()[:, c, tt * 128:(tt + 1) * 128], identity=IDF[:, :]),
                                  r=[FN.reg(c * 512 + tt * 128, c * 512 + (tt + 1) * 128), IDF.reg()], w=[psr(bank, cc * 128, (cc + 1) * 128)], sig=(cc == 3))
                        if hb == 0:
                            fw.op("act", S_.copy, dict(out=os_.ap()[:, 0:512], in_=psb(bank)), r=[psr(bank)], w=[os_.reg(0, 512)])
                        else:
                            fw.op("dve", V.tensor_copy, dict(out=os_.ap()[:, 512:1024], in_=psb(bank)), r=[psr(bank)], w=[os_.reg(512, 1024)])
                    out_events.append(fw.dma("sp", out_d[t * 128:(t + 1) * 128, :], os_.ap(), r=[os_.reg()]))
        if dbg:
            for c in range(8):
                out_events.append(fw.dma("sp", dbg_d[:, c * NTOK:(c + 1) * NTOK], XT[:, c, :], r=[xt_reg(c, 0, NTOK)]))
        for ev in out_events:
            fw.wait_event("sp", ev)
        build.stats = dict(nops=dict(fw.nops), nwaits=fw.nwaits, cnt=dict(fw.cnt))
    return nc


def _rope_tables():
    t = np.arange(SEQ)
    row = (t // 64).astype(np.float32)
    col = (t % 64).astype(np.float32)

    def tab(dim):
        nf = dim // 4
        inv = (10000.0 ** (-np.arange(nf, dtype=np.float32) / nf)).astype(np.float32)
        ang = np.concatenate([row[:, None] * inv, col[:, None] * inv], axis=-1).astype(np.float32)
        return np.cos(ang).astype(np.float32), np.sin(ang).astype(np.float32)

    ca, sa = tab(64)
    cc, sc = tab(32)
    return np.ascontiguousarray(np.concatenate([ca, sa, cc, sc], axis=1), dtype=np.float32)


def _mask_a():
    k = np.arange(128)[:, None]
    q = np.arange(128)[None, :]
    prev = np.where(q <= k, 0.0, NEG).astype(np.float32)
    nxt = np.where(k <= q, 0.0, NEG).astype(np.float32)
    return np.ascontiguousarray(np.concatenate([prev, prev, nxt, nxt], axis=1), dtype=np.float32)


def _na_bias(rpb):
    out = np.full((4, 21, 128, 128), NEG, dtype=np.float32)
    kk = np.arange(128)
    kr_l, kc = kk // 64, kk % 64
    qq = np.arange(128)
    qr_l, qc = qq // 64, qq % 64
    cs = np.clip(qc - 8, 0, 64 - 16)
    col_valid = (kc[:, None] >= cs[None, :]) & (kc[:, None] < cs[None, :] + 16)
    dc = np.clip(kc[:, None] - qc[None, :], -15, 15) + 15

    def block(i, j):
        krow = 2 * j + kr_l
        qrow = 2 * i + qr_l
        rs = np.clip(qrow - 4, 0, 24)
        row_valid = (krow[:, None] >= rs[None, :]) & (krow[:, None] < rs[None, :] + 8)
        dr = np.clip(krow[:, None] - qrow[None, :] + 7, 0, 14)
        valid = row_valid & col_valid
        return valid, dr

    for i in list(NA_EDGE) + [5]:
        for j in na_ktiles(i):
            bidx = na_block_index(i, j)
            valid, dr = block(i, j)
            for h in range(4):
                g = rpb[h][dr, dc]
                out[h, bidx] = np.where(valid, g, np.float32(NEG))
    o = out.reshape(2, 2, 21, 128, 128).transpose(0, 3, 1, 2, 4).reshape(2, 128, 2 * 21 * 128)
    return np.ascontiguousarray(o, dtype=np.float32)


_NC_CACHE = {}


def _prep_shared(inp):
    f = lambda a: np.ascontiguousarray(np.asarray(a, dtype=np.float32))
    w_in = f(inp["w_in"]).copy()
    qa = w_in[:, :, 0:256].reshape(NL, D, 4, 64)
    w_in[:, :, 0:256] = qa[:, :, [0, 2, 1, 3], :].reshape(NL, D, 256)
    bvec = np.zeros((NL, 196), np.float32)
    for l in range(NL):
        bvec[l, 0:4] = f(inp["attn_sink"])[l]
        bvec[l, 4:36] = f(inp["diff_lq1"])[l]
        bvec[l, 36:68] = f(inp["diff_lk1"])[l]
        bvec[l, 68:100] = f(inp["diff_lq2"])[l]
        bvec[l, 100:132] = f(inp["diff_lk2"])[l]
        bvec[l, 132:196] = f(inp["diff_subln_g"])[l]
    nab = np.stack([_na_bias(f(inp["na_rpb"])[l]) for l in range(NL)], axis=0)
    shared = dict(bvec=bvec.reshape(-1), rope=_rope_tables(), maska=_mask_a(), nab=np.ascontiguousarray(nab),
                  w_ada=f(inp["w_ada"]), w_in=w_in, w_out=f(inp["w_out"]), w_gate=f(inp["w_gate"]), w_up=f(inp["w_up"]),
                  w_down=f(inp["w_down"]))
    rows = np.zeros((384, 128), np.float32)
    rows[8:16] = f(inp["c_ctx"]).reshape(8, 128)
    for l in range(NL):
        rows[16 + 8 * l:24 + 8 * l] = f(inp["norm1_g"])[l].reshape(8, 128)
        rows[32 + 8 * l:40 + 8 * l] = f(inp["norm2_g"])[l].reshape(8, 128)
        rows[48 + 48 * l:96 + 48 * l] = f(inp["b_ada"])[l].reshape(48, 128)
        rows[152 + 2 * l:154 + 2 * l] = f(inp["conv_b"])[l].reshape(2, 128)
        rows[156 + 2 * l:158 + 2 * l] = f(inp["conv_ln_g"])[l].reshape(2, 128)
        rows[160 + 2 * l:162 + 2 * l] = f(inp["conv_ln_b"])[l].reshape(2, 128)
        rows[164 + 62 * l:226 + 62 * l] = f(inp["conv_w"])[l].reshape(31, 256).reshape(62, 128)
    rows[144:152] = f(inp["final_g"]).reshape(8, 128)
    pp = np.arange(128)
    rows[288] = ((pp % 64) < 32).astype(np.float32)
    rows[289] = ((pp % 64) >= 32).astype(np.float32)
    return shared, rows


def make_in_maps(inp, cores):
    shared, rows = _prep_shared(inp)
    x = np.asarray(inp["x"], dtype=np.float32)
    ctx = np.asarray(inp["ctx"], dtype=np.float32)
    c = np.asarray(inp["c"], dtype=np.float32)
    maps = []
    for b in cores:
        r = rows.copy()
        r[0:8] = c[b].reshape(8, 128)
        m = dict(shared)
        m.update(x=np.ascontiguousarray(x[b]), ctx=np.ascontiguousarray(ctx[b]), vecs=r)
        maps.append(m)
    return maps


def kernel(**inputs):
    if "nc" not in _NC_CACHE:
        _NC_CACHE["nc"] = build()
    nc = _NC_CACHE["nc"]
    in_maps = make_in_maps(inputs, list(range(8)))
    res = run_bass_kernel_spmd(nc, in_maps, core_ids=list(range(8)))
    out = np.stack([np.asarray(r["out"], dtype=np.float32) for r in res.results], axis=0)
    return out
```

```python
import math
from contextlib import ExitStack
import numpy as np
import concourse.bass as bass
import concourse.mybir as mybir
from concourse.bass_utils import run_bass_kernel_spmd

F32 = mybir.dt.float32
BF16 = mybir.dt.bfloat16
AF = mybir.ActivationFunctionType
ALU = mybir.AluOpType
CELL = 64

D = 1024
SEQ = 2048
CTX = 256
NTOK = SEQ + CTX
NT = NTOK // 128
NL = 2
FH = 2816
EPS = 1e-6
TB = [(0, 512), (512, 512), (1024, 512), (1536, 512), (2048, 256)]
NEG = -30000.0
OPTS = {"a_stage": 99, "c_stage": 99}


class Reg:
    __slots__ = ("name", "lo", "hi")

    def __init__(self, name, lo, hi):
        self.name, self.lo, self.hi = name, int(lo), int(hi)


class TT:
    def __init__(self, fw, name, shape, dtype, space="sbuf"):
        self.fw, self.name, self.shape, self.dtype = fw, name, list(shape), dtype
        cm = fw.nc.sbuf_tensor(name, self.shape, dtype) if space == "sbuf" else fw.nc.psum_tensor(name, self.shape, dtype)
        self.h = fw.stack.enter_context(cm)
        self.free = int(np.prod(self.shape[1:]))

    def reg(self, lo=0, hi=None):
        return Reg(self.name, lo, self.free if hi is None else hi)

    def __getitem__(self, idx):
        return self.h[idx]


class AV:
    def __init__(self, ar, off, shape, dtype=BF16):
        self.ar, self.off, self.shape, self.dtype = ar, int(off), list(shape), dtype
        self.n = int(np.prod(shape))
        self.k = 2 if dtype == F32 else 1
        assert off % CELL == 0, off
        assert self.off + self.n * self.k <= ar.free, (off, shape)

    def ap(self):
        a = self.ar.h[:, self.off:self.off + self.n * self.k]
        if self.dtype == F32:
            a = a.bitcast(F32)
        if len(self.shape) > 1:
            names = " ".join("d%d" % i for i in range(len(self.shape)))
            kw = {"d%d" % i: self.shape[i] for i in range(1, len(self.shape))}
            a = a.rearrange("p (%s) -> p %s" % (names, names), **kw)
        return a

    def reg(self, lo=0, hi=None):
        hi = self.n if hi is None else hi
        return Reg(self.ar.name, self.off + lo * self.k, self.off + hi * self.k)


class FW:
    def __init__(self, nc, stack, n_dma_sems=10):
        self.nc, self.stack = nc, stack
        self.eng = {"pe": nc.tensor, "act": nc.scalar, "dve": nc.vector, "pool": nc.gpsimd, "sp": nc.sync}
        self.sem, self.cnt = {}, {}
        for e in ("pe", "act", "dve", "pool"):
            self.sem[e] = stack.enter_context(nc.semaphore("s_" + e))
            self.cnt[e] = 0
        self.dsem, self.dcnt, self.dnext = {}, {}, {}
        for q in ("sp", "pool"):
            self.dsem[q] = [stack.enter_context(nc.semaphore("d_%s%d" % (q, i))) for i in range(n_dma_sems)]
            self.dcnt[q] = [0] * n_dma_sems
            self.dnext[q] = 0
        self.waited = {}
        self.cells = {}
        self.psbank = {}
        self.nwaits = 0
        self.nops = {"pe": 0, "act": 0, "dve": 0, "pool": 0, "sp": 0}

    def _semof(self, key):
        return self.dsem[key[1]][key[2]] if isinstance(key, tuple) else self.sem[key]

    @staticmethod
    def _need(needs, ev):
        if ev is not None and needs.get(ev[0], 0) < ev[1]:
            needs[ev[0]] = ev[1]

    def _collect(self, r, w):
        needs = {}
        for g in r:
            for c in range(g.lo // CELL, (g.hi - 1) // CELL + 1):
                st = self.cells.get((g.name, c))
                if st is not None:
                    self._need(needs, st[0])
        for g in w:
            for c in range(g.lo // CELL, (g.hi - 1) // CELL + 1):
                st = self.cells.get((g.name, c))
                if st is not None:
                    self._need(needs, st[0])
                    for k, v in st[1].items():
                        self._need(needs, (k, v))
        return needs

    def _emit_waits(self, e, needs, skip_self=False):
        engine = self.eng[e]
        for k, v in needs.items():
            if skip_self and k == e:
                continue
            if self.waited.get((e, k), 0) >= v:
                continue
            engine.wait_ge(self._semof(k), v)
            self.waited[(e, k)] = v
            self.nwaits += 1

    def _stamp(self, ev, r, w):
        k, v = ev
        for g in r:
            for c in range(g.lo // CELL, (g.hi - 1) // CELL + 1):
                st = self.cells.setdefault((g.name, c), [None, {}])
                if st[1].get(k, 0) < v:
                    st[1][k] = v
        for g in w:
            for c in range(g.lo // CELL, (g.hi - 1) // CELL + 1):
                self.cells[(g.name, c)] = [ev, {}]

    def op(self, e, fn, kw, r=(), w=(), sig=True):
        needs = self._collect(r, w)
        banks = set()
        for g in list(r) + list(w):
            if g.name == "PS":
                banks.update(range(g.lo // 512, (g.hi - 1) // 512 + 1))
        for b in banks:
            for e2, v in self.psbank.get(b, {}).items():
                if e2 != e:
                    self._need(needs, (e2, v))
        self._emit_waits(e, needs, skip_self=(e == "pe"))
        ins = fn(**kw)
        ev = (e, self.cnt[e] + 1)
        if sig:
            ins.then_inc(self.sem[e], 1)
            self.cnt[e] += 1
        self._stamp(ev, r, w)
        for b in banks:
            self.psbank.setdefault(b, {})[e] = ev[1]
        self.nops[e] += 1
        return ins

    def dma(self, q, out, in_, r=(), w=(), **kw):
        needs = self._collect(r, w)
        i = self.dnext[q]
        self.dnext[q] = (i + 1) % len(self.dsem[q])
        key = ("d", q, i)
        if self.dcnt[q][i] > 0:
            self._need(needs, (key, self.dcnt[q][i]))
        self._emit_waits(q, needs)
        self.dcnt[q][i] += 16
        ins = self.eng[q].dma_start(out=out, in_=in_, **kw)
        ins.then_inc(self.dsem[q][i], 16)
        ev = (key, self.dcnt[q][i])
        self._stamp(ev, r, w)
        self.nops[q] += 1
        return ev

    def wait_event(self, e, ev):
        self._emit_waits(e, {ev[0]: ev[1]})


def lambda_init(l):
    return 0.8 - 0.6 * math.exp(-0.3 * l)


def na_ktiles(i):
    r0, r1 = 2 * i, 2 * i + 1
    rs0 = min(max(r0 - 4, 0), 24)
    rs1 = min(max(r1 - 4, 0), 24)
    return list(range(rs0 // 2, (rs1 + 7) // 2 + 1))


NA_EDGE = (0, 1, 14, 15)


def na_block_index(i, j):
    if i in NA_EDGE:
        e = NA_EDGE.index(i)
        kl = na_ktiles(i)
        return 5 + e * 4 + kl.index(j)
    return (j - i) + 2


def build(layers=(0, 1), mixers="ABCD", do_ffn=True, do_final=True, dbg=False):
    nc = bass.Bass("TRN2", target_bir_lowering=False)

    def din(name, shape):
        return nc.dram_tensor(name, list(shape), F32, kind="ExternalInput").ap()

    x_d = din("x", [SEQ, D])
    ctx_d = din("ctx", [CTX, D])
    vecs_d = din("vecs", [384, 128])
    bvec_d = din("bvec", [392])
    rope_d = din("rope", [SEQ, 96])
    maska_d = din("maska", [128, 512])
    nab_d = din("nab", [NL, 2, 128, 2 * 21 * 128])
    w_ada_d = din("w_ada", [NL, D, 6 * D])
    w_in_d = din("w_in", [NL, D, 2560])
    w_out_d = din("w_out", [NL, D, D])
    w_gate_d = din("w_gate", [NL, D, FH])
    w_up_d = din("w_up", [NL, D, FH])
    w_down_d = din("w_down", [NL, FH, D])
    out_d = nc.dram_tensor("out", [SEQ, D], F32, kind="ExternalOutput").ap()
    dbg_d = nc.dram_tensor("dbg", [128, 8 * NTOK], F32, kind="ExternalOutput").ap() if dbg else None

    with ExitStack() as stack:
        fw = FW(nc, stack)
        V, S_, PE_, PL = nc.vector, nc.scalar, nc.tensor, nc.gpsimd

        XT = TT(fw, "XT", [128, 8, NTOK], F32)
        HT = TT(fw, "HT", [128, 8, NTOK], BF16)
        VEC = TT(fw, "VEC", [128, 384], F32)
        BV = TT(fw, "BV", [128, 392], F32)
        IDF = TT(fw, "IDF", [128, 128], F32)
        IDB = TT(fw, "IDB", [128, 128], BF16)
        ONF = TT(fw, "ONF", [128, 128], F32)
        ONB = TT(fw, "ONB", [128, 128], BF16)
        ROPE = TT(fw, "ROPE", [128, 16, 96], F32)
        MASKA = TT(fw, "MASKA", [128, 512], BF16)
        MOD = TT(fw, "MOD", [128, NL, 48, 2], F32)
        GP = TT(fw, "GP", [128, NL, 2, 8, 2], F32)
        SC2 = TT(fw, "SC2", [128, 8, 2], BF16)
        SM = TT(fw, "SM", [128, 256], F32)
        AR = TT(fw, "AR", [128, 41600], BF16)
        PS = TT(fw, "PS", [128, 4096], F32, space="psum")

        ESINK = 0
        NEGLAM = 8
        EPSC = 10
        SGV = 16
        TMPS = 160

        def sm(a, b):
            return SM[:, a:b]

        def smr(a, b):
            return SM.reg(a, b)

        def psb(bank, a=0, b=512):
            return PS[:, bank * 512 + a: bank * 512 + b]

        def psr(bank, a=0, b=512):
            return PS.reg(bank * 512 + a, bank * 512 + b)

        def psb16(bank, a=0, b=1024):
            return PS[:, bank * 512: (bank + 1) * 512].bitcast(BF16)[:, a:b]

        def psr16(bank, a=0, b=1024):
            return PS.reg(bank * 512 + a // 2, bank * 512 + (b + 1) // 2)

        def xt_reg(c, t0, n):
            return XT.reg(c * NTOK + t0, c * NTOK + t0 + n)

        def ht_reg(c, t0, n):
            return HT.reg(c * NTOK + t0, c * NTOK + t0 + n)

        fw.op("pool", PL.memset, dict(ap=IDF[:, :], constant=0.0), w=[IDF.reg()])
        fw.op("pool", PL.affine_select, dict(out=IDF[:, :], in_=IDF[:, :], pattern=[[-1, 128]], compare_op=ALU.not_equal,
                                             fill=1.0, base=0, channel_multiplier=1), r=[IDF.reg()], w=[IDF.reg()])
        fw.op("dve", V.tensor_copy, dict(out=IDB[:, :], in_=IDF[:, :]), r=[IDF.reg()], w=[IDB.reg()])
        fw.op("dve", V.memset, dict(ap=ONF[:, :], constant=1.0), w=[ONF.reg()])
        fw.op("dve", V.memset, dict(ap=ONB[:, :], constant=1.0), w=[ONB.reg()])
        fw.op("dve", V.memset, dict(ap=SM[:, :], constant=0.0), w=[SM.reg()])
        fw.op("dve", V.memset, dict(ap=sm(EPSC, EPSC + 1), constant=EPS), w=[smr(EPSC, EPSC + 1)])

        fw.dma("sp", BV[:, :], bvec_d.partition_broadcast(128), w=[BV.reg()])
        fw.dma("sp", ROPE[:, :, :], rope_d.rearrange("(t p) f -> p t f", p=128), w=[ROPE.reg()])
        fw.dma("pool", MASKA[:, :], maska_d, w=[MASKA.reg()])

        for i in range(3):
            stg = AV(AR, 2048 * (i % 2), [128], F32)
            fw.dma("sp", stg.ap(), vecs_d[i * 128:(i + 1) * 128, :], w=[stg.reg()])
            fw.op("pe", PE_.transpose, dict(out=psb(i, 0, 128), in_=stg.ap(), identity=IDF[:, :]),
                  r=[stg.reg(), IDF.reg()], w=[psr(i, 0, 128)])
            fw.op("dve", V.tensor_copy, dict(out=VEC[:, i * 128:(i + 1) * 128], in_=psb(i, 0, 128)),
                  r=[psr(i, 0, 128)], w=[VEC.reg(i * 128, (i + 1) * 128)])
        fw.op("act", S_.activation, dict(out=SC2[:, :, :].rearrange("p k s -> p s k"),
                                         in_=VEC[:, 0:16].rearrange("p (s k) -> p s k", s=2), func=AF.Silu),
              r=[VEC.reg(0, 16)], w=[SC2.reg()])
        for l in range(NL):
            b0 = 196 * l
            li = lambda_init(l)
            fw.op("act", S_.activation, dict(out=sm(ESINK + 4 * l, ESINK + 4 * l + 4), in_=BV[:, b0:b0 + 4], func=AF.Exp),
                  r=[BV.reg(b0, b0 + 4)], w=[smr(ESINK + 4 * l, ESINK + 4 * l + 4)])
            for m in range(2):
                o = b0 + 4 + 64 * m
                fw.op("dve", V.tensor_tensor, dict(out=sm(TMPS, TMPS + 32), in0=BV[:, o:o + 32], in1=BV[:, o + 32:o + 64], op=ALU.mult),
                      r=[BV.reg(o, o + 64)], w=[smr(TMPS, TMPS + 32)])
                fw.op("dve", V.tensor_reduce, dict(out=sm(TMPS + 40 + m, TMPS + 41 + m), in_=sm(TMPS, TMPS + 32),
                                                   axis=mybir.AxisListType.X, op=ALU.add),
                      r=[smr(TMPS, TMPS + 32)], w=[smr(TMPS + 40 + m, TMPS + 41 + m)])
            fw.op("act", S_.activation, dict(out=sm(TMPS + 48, TMPS + 50), in_=sm(TMPS + 40, TMPS + 42), func=AF.Exp),
                  r=[smr(TMPS + 40, TMPS + 42)], w=[smr(TMPS + 48, TMPS + 50)])
            fw.op("dve", V.tensor_tensor, dict(out=sm(TMPS + 52, TMPS + 53), in0=sm(TMPS + 49, TMPS + 50), in1=sm(TMPS + 48, TMPS + 49),
                                               op=ALU.subtract),
                  r=[smr(TMPS + 48, TMPS + 50)], w=[smr(TMPS + 52, TMPS + 53)])
            fw.op("dve", V.tensor_scalar, dict(out=sm(NEGLAM + l, NEGLAM + l + 1), in0=sm(TMPS + 52, TMPS + 53), scalar1=-li, scalar2=None,
                                               op0=ALU.add),
                  r=[smr(TMPS + 52, TMPS + 53)], w=[smr(NEGLAM + l, NEGLAM + l + 1)])
            fw.op("dve", V.tensor_scalar, dict(out=sm(SGV + 64 * l, SGV + 64 * l + 64), in0=BV[:, b0 + 132:b0 + 196], scalar1=1.0 - li,
                                               scalar2=None, op0=ALU.mult),
                  r=[BV.reg(b0 + 132, b0 + 196)], w=[smr(SGV + 64 * l, SGV + 64 * l + 64)])

        def norm(gp_ap, sh_ap, blocks, out_fn, extra_r=(), after_b=None):
            def bufs(bi):
                nb0 = 36864 if bi % 2 == 0 else 32768
                SQ = [AV(AR, nb0 + 512 * i, [512], BF16) for i in range(2)]
                RSTD = AV(AR, nb0 + 1024, [512], F32)
                TMP = [AV(AR, nb0 + 2048 + 1024 * i, [512], F32) for i in range(2)]
                return SQ, RSTD, TMP, 7 - (bi % 2)

            def stage_a(bi):
                t0, n = TB[bi]
                SQ, RSTD, TMP, bank = bufs(bi)
                for c in range(8):
                    if c % 2 == 0:
                        fw.op("act", S_.activation, dict(out=SQ[c % 2].ap()[:, :n], in_=XT[:, c, t0:t0 + n], func=AF.Square),
                              r=[xt_reg(c, t0, n)], w=[SQ[c % 2].reg(0, n)])
                    else:
                        fw.op("dve", V.tensor_tensor, dict(out=SQ[c % 2].ap()[:, :n], in0=XT[:, c, t0:t0 + n], in1=XT[:, c, t0:t0 + n], op=ALU.mult),
                              r=[xt_reg(c, t0, n)], w=[SQ[c % 2].reg(0, n)])
                    fw.op("pe", PE_.matmul, dict(out=psb(bank, 0, n), lhsT=ONB[:, :], rhs=SQ[c % 2].ap()[:, :n], start=(c == 0), stop=(c == 7)),
                          r=[ONB.reg(), SQ[c % 2].reg(0, n)], w=[psr(bank, 0, n)])
                fw.op("act", S_.activation, dict(out=RSTD.ap()[:, :n], in_=psb(bank, 0, n), func=AF.Ln, bias=sm(EPSC, EPSC + 1), scale=1.0 / D),
                      r=[psr(bank, 0, n), smr(EPSC, EPSC + 1)], w=[RSTD.reg(0, n)])
                fw.op("act", S_.activation, dict(out=RSTD.ap()[:, :n], in_=RSTD.ap()[:, :n], func=AF.Exp, scale=-0.5), r=[RSTD.reg(0, n)], w=[RSTD.reg(0, n)])

            def stage_b(bi):
                t0, n = TB[bi]
                s = 1 if bi == 4 else 0
                SQ, RSTD, TMP, bank = bufs(bi)
                for c in range(8):
                    fw.op("dve", V.tensor_tensor, dict(out=TMP[c % 2].ap()[:, :n], in0=XT[:, c, t0:t0 + n], in1=RSTD.ap()[:, :n], op=ALU.mult),
                          r=[xt_reg(c, t0, n), RSTD.reg(0, n)], w=[TMP[c % 2].reg(0, n)])
                    o_ap, o_regs = out_fn(c, bi)
                    kw = dict(out=o_ap, in_=TMP[c % 2].ap()[:, :n], func=AF.Identity, scale=gp_ap(c, s))
                    rr = [TMP[c % 2].reg(0, n), VEC.reg()] + list(extra_r)
                    if sh_ap is not None:
                        kw["bias"] = sh_ap(c, s)
                    fw.op("act", S_.activation, kw, r=rr, w=o_regs)

            stage_a(blocks[0])
            for k, bi in enumerate(blocks):
                if k + 1 < len(blocks):
                    stage_a(blocks[k + 1])
                stage_b(bi)
                if after_b is not None:
                    after_b(bi)

        def ht_out(c, bi):
            t0, n = TB[bi]
            return HT[:, c, t0:t0 + n], [ht_reg(c, t0, n)]

        def ada_steps(l, base=0, bank=6):
            WA = [AV(AR, base + 4096 * i, [8, 512], BF16) for i in range(2)]
            wv = w_ada_d[l].rearrange("(k p) n -> p k n", p=128)

            def dma_step(g):
                def f():
                    wa = WA[g % 2]
                    fw.dma("pool", wa.ap(), wv[:, :, g * 512:(g + 1) * 512], w=[wa.reg()])
                return f

            def mm_step(g):
                def f():
                    wa = WA[g % 2]
                    for jj in range(4):
                        j = g * 4 + jj
                        for kc in range(8):
                            fw.op("pe", PE_.matmul, dict(out=psb(bank, 2 * j, 2 * j + 2), lhsT=wa.ap()[:, kc, jj * 128:(jj + 1) * 128],
                                                         rhs=SC2[:, kc, :], start=(kc == 0), stop=(kc == 7)),
                                  r=[wa.reg(kc * 512 + jj * 128, kc * 512 + (jj + 1) * 128), SC2.reg()], w=[psr(bank, 2 * j, 2 * j + 2)],
                                  sig=(kc == 7))
                return f

            def fin():
                fw.op("dve", V.tensor_tensor, dict(out=MOD[:, l, :, :], in0=psb(bank, 0, 96).rearrange("p (j s) -> p j s", s=2),
                                                   in1=VEC[:, 48 + 48 * l:96 + 48 * l].unsqueeze(2).broadcast_to([128, 48, 2]), op=ALU.add),
                      r=[psr(bank, 0, 96), VEC.reg()], w=[MOD.reg(l * 96, (l + 1) * 96)])
                for n_ in range(2):
                    scl = MOD[:, l, 8 + 24 * n_:16 + 24 * n_, :]
                    g_ap = VEC[:, 16 + 16 * n_ + 8 * l:24 + 16 * n_ + 8 * l].unsqueeze(2).broadcast_to([128, 8, 2])
                    fw.op("dve", V.tensor_scalar, dict(out=GP[:, l, n_, :, :], in0=scl, scalar1=1.0, scalar2=None, op0=ALU.add),
                          r=[MOD.reg(l * 96, (l + 1) * 96)], w=[GP.reg(l * 32 + n_ * 16, l * 32 + (n_ + 1) * 16)])
                    fw.op("dve", V.tensor_tensor, dict(out=GP[:, l, n_, :, :], in0=GP[:, l, n_, :, :], in1=g_ap, op=ALU.mult),
                          r=[GP.reg(l * 32 + n_ * 16, l * 32 + (n_ + 1) * 16), VEC.reg()], w=[GP.reg(l * 32 + n_ * 16, l * 32 + (n_ + 1) * 16)])
            return [dma_step(g) for g in range(12)], [mm_step(g) for g in range(12)], fin

        def ada(l):
            dmas, mms, fin = ada_steps(l)
            for g in range(12):
                dmas[g]()
                mms[g]()
            fin()

        WM = AV(AR, 0, [8, 768])
        WO = AV(AR, 6144, [2, 1024])
        YT = AV(AR, 8192, [2, NTOK])
        QKS = [AV(AR, 12800 + 512 * i, [512]) for i in range(2)]
        YTILE = [AV(AR, 13824 + 1024 * i, [4, 256]) for i in range(2)]
        RT = [AV(AR, 15872 + 512 * i, [256], F32) for i in range(4)]
        PPT = AV(AR, 17920, [1024], F32)
        SP0 = 19968
        QK = AV(AR, SP0, [4, NTOK])
        VB = AV(AR, SP0 + 9216, [NT, 4, 65])
        PT0 = SP0 + 9216 + 4736

        def load_wm(l, c0, ncols):
            wv = w_in_d[l].rearrange("(k p) n -> p k n", p=128)
            for h in range(2):
                fw.dma("pool", WM.ap()[:, 4 * h:4 * h + 4, 0:ncols], wv[:, 4 * h:4 * h + 4, c0:c0 + ncols],
                       w=[WM.reg(4 * h * 768, (4 * h + 4) * 768)])

        def load_wo(l, m):
            wv = w_out_d[l][m * 256:(m + 1) * 256, :].rearrange("(k p) n -> p k n", p=128)
            fw.dma("pool", WO.ap(), wv, w=[WO.reg()])

        def inproj(l, ncols, evac):
            deferred = None
            for t in range(NT):
                banks = [(t % 2) * 2, (t % 2) * 2 + 1]
                nb = (ncols + 511) // 512
                for b in range(nb):
                    a_, b_ = b * 512, min(ncols, (b + 1) * 512)
                    for kc in range(8):
                        fw.op("pe", PE_.matmul, dict(out=psb(banks[b], 0, b_ - a_), lhsT=HT[:, kc, t * 128:(t + 1) * 128],
                                                     rhs=WM.ap()[:, kc, a_:b_], start=(kc == 0), stop=(kc == 7)),
                              r=[ht_reg(kc, t * 128, 128), WM.reg(kc * 768 + a_, kc * 768 + b_)], w=[psr(banks[b], 0, b_ - a_)],
                              sig=(kc == 7))
                nj = OPTS.get("junk", 0)
                for q_ in range(nj):
                    fw.op("pe", PE_.matmul, dict(out=psb(7, 0, 512), lhsT=IDB[:, :], rhs=HT[:, q_ % 8, 0:512], start=True, stop=True),
                          r=[IDB.reg(), ht_reg(q_ % 8, 0, 512)], w=[psr(7, 0, 512)], sig=(q_ == nj - 1))
                if deferred is not None:
                    deferred()
                deferred = evac(t, banks)
            if deferred is not None:
                deferred()

        def rope(t, src_ap, src_reg, U, Fh, tab_off, dst, dst_off):
            x = src_ap.rearrange("p (u h f) -> p u h f", u=U, h=2)
            o = dst.ap()[:, dst_off:dst_off + U * 2 * Fh].rearrange("p (u h f) -> p u h f", u=U, h=2)
            cs = ROPE[:, t, tab_off:tab_off + Fh].unsqueeze(1).broadcast_to([128, U, Fh])
            sn = ROPE[:, t, tab_off + Fh:tab_off + 2 * Fh].unsqueeze(1).broadcast_to([128, U, Fh])
            n = U * Fh
            tv = [RT[i].ap()[:, :n].rearrange("p (u f) -> p u f", u=U) for i in range(4)]
            tr = [RT[i].reg(0, n) for i in range(4)]
            rr = [src_reg, ROPE.reg()]
            fw.op("dve", V.tensor_tensor, dict(out=tv[0], in0=x[:, :, 0, :], in1=cs, op=ALU.mult), r=rr, w=[tr[0]])
            fw.op("dve", V.tensor_tensor, dict(out=tv[1], in0=x[:, :, 1, :], in1=sn, op=ALU.mult), r=rr, w=[tr[1]])
            fw.op("dve", V.tensor_tensor, dict(out=tv[2], in0=x[:, :, 0, :], in1=sn, op=ALU.mult), r=rr, w=[tr[2]])
            fw.op("dve", V.tensor_tensor, dict(out=tv[3], in0=x[:, :, 1, :], in1=cs, op=ALU.mult), r=rr, w=[tr[3]])
            dreg = dst.reg(dst_off, dst_off + U * 2 * Fh)
            fw.op("pool", PL.tensor_tensor, dict(out=o[:, :, 0, :], in0=tv[0], in1=tv[1], op=ALU.subtract), r=[tr[0], tr[1]], w=[dreg])
            fw.op("pool", PL.tensor_tensor, dict(out=o[:, :, 1, :], in0=tv[2], in1=tv[3], op=ALU.add), r=[tr[2], tr[3]], w=[dreg])

        def qk_transposes(t, qs, nch, dst, dst_tok0):
            bank = 4 + (t % 2)
            for c in range(nch):
                fw.op("pe", PE_.transpose, dict(out=psb16(bank, c * 128, (c + 1) * 128), in_=qs.ap()[:, c * 128:(c + 1) * 128], identity=IDB[:, :]),
                      r=[qs.reg(c * 128, (c + 1) * 128), IDB.reg()], w=[psr16(bank, c * 128, (c + 1) * 128)], sig=(c == nch - 1))
            W_ = dst.shape[1]
            fw.op("act", S_.copy, dict(out=dst.ap()[:, 0:nch, dst_tok0:dst_tok0 + 128],
                                       in_=psb16(bank, 0, nch * 128).rearrange("p (c t) -> p c t", c=nch)),
                  r=[psr16(bank, 0, nch * 128)], w=[dst.reg(c * W_ + dst_tok0, c * W_ + dst_tok0 + 128) for c in range(nch)])

        def v_evac(t, src_ap, src_reg, H, vb=None):
            vb = VB if vb is None else vb
            fw.op("act", S_.copy, dict(out=vb.ap()[:, t, 0:H, 0:64], in_=src_ap.rearrange("p (h d) -> p h d", h=H)),
                  r=[src_reg], w=[vb.reg(t * 260, t * 260 + H * 65)])

        def vb_ones(vb=None):
            vb = VB if vb is None else vb
            fw.op("dve", V.memset, dict(ap=vb.ap()[:, :, :, 64:65], constant=1.0), w=[vb.reg()])

        def y_transposes(yt, ntile_list, bank, col0):
            for (slot, tok0) in ntile_list:
                for c in range(2):
                    fw.op("pe", PE_.transpose, dict(out=psb16(bank, col0 + c * 128, col0 + (c + 1) * 128),
                                                    in_=yt.ap()[:, slot, c * 128:(c + 1) * 128], identity=IDB[:, :]),
                          r=[yt.reg(slot * 256 + c * 128, slot * 256 + (c + 1) * 128), IDB.reg()],
                          w=[psr16(bank, col0 + c * 128, col0 + (c + 1) * 128)], sig=(c == 1))
                fw.op("act", S_.copy, dict(out=YT.ap()[:, 0:2, tok0:tok0 + 128],
                                           in_=psb16(bank, col0, col0 + 256).rearrange("p (c t) -> p c t", c=2)),
                      r=[psr16(bank, col0, col0 + 256)], w=[YT.reg(c * NTOK + tok0, c * NTOK + tok0 + 128) for c in range(2)])

        oc_rot = [0]

        def outproj(l, blocks):
            for bi in blocks:
                t0, n = TB[bi]
                s = 1 if bi == 4 else 0
                for oc in range(8):
                    bank = oc_rot[0] % 8
                    oc_rot[0] += 1
                    for kc in range(2):
                        fw.op("pe", PE_.matmul, dict(out=psb(bank, 0, n), lhsT=WO.ap()[:, kc, oc * 128:(oc + 1) * 128], rhs=YT.ap()[:, kc, t0:t0 + n],
                                                     start=(kc == 0), stop=(kc == 1)),
                              r=[WO.reg(kc * 1024 + oc * 128, kc * 1024 + (oc + 1) * 128), YT.reg(kc * NTOK + t0, kc * NTOK + t0 + n)],
                              w=[psr(bank, 0, n)], sig=(kc == 1))
                    fw.op("dve", V.scalar_tensor_tensor, dict(out=XT[:, oc, t0:t0 + n], in0=psb(bank, 0, n), scalar=MOD[:, l, 16 + oc, s:s + 1],
                                                              in1=XT[:, oc, t0:t0 + n], op0=ALU.mult, op1=ALU.add),
                          r=[psr(bank, 0, n), MOD.reg(l * 96, (l + 1) * 96), xt_reg(oc, t0, n)], w=[xt_reg(oc, t0, n)])

        def mixer_A(l, ctx_needed):
            load_wm(l, 0, 512)
            load_wo(l, 0)
            vb_ones()

            def evac(t, banks):
                b = banks[0]
                qs = QKS[t % 2]
                if t < 16:
                    rope(t, psb(b, 0, 384), psr(b, 0, 384), 6, 32, 0, qs, 0)
                else:
                    fw.op("dve", V.tensor_copy, dict(out=qs.ap()[:, 0:384], in_=psb(b, 0, 384)), r=[psr(b, 0, 384)], w=[qs.reg(0, 384)])
                v_evac(t, psb(b, 384, 512), psr(b, 384, 512), 2)
                return lambda: qk_transposes(t, qs, 3, QK, t * 128)

            inproj(l, 512, evac)
            if OPTS["a_stage"] < 2:
                return
            PT = [AV(AR, PT0 + 1280 * i, [1280]) for i in range(2)]
            qtiles = list(range(16)) + ([16, 17] if ctx_needed else [])
            units = []
            for i in qtiles:
                for g in range(2):
                    units.append((i, g, len(units)))

            def ktl_of(i):
                if i < 16:
                    return [(j, (0 if j == i - 1 else 1 if j == i + 1 else None)) for j in (i - 1, i, i + 1) if 0 <= j < 16] + [(16, None), (17, None)]
                return [(16, None), (17, None)]

            def S_part(u):
                i, g, unit = u
                ktl = ktl_of(i)
                nk = len(ktl)
                sb = (unit % 2) * 1536
                for jj, (j, mk) in enumerate(ktl):
                    o_ap = PS[:, sb + jj * 256: sb + (jj + 1) * 256]
                    o_rg = PS.reg(sb + jj * 256, sb + (jj + 1) * 256)
                    fw.op("pe", PE_.matmul, dict(out=o_ap, lhsT=QK.ap()[64 * g:64 * g + 64, 2, j * 128:(j + 1) * 128],
                                                 rhs=QK.ap()[64 * g:64 * g + 64, 0:2, i * 128:(i + 1) * 128], start=True, stop=(mk is None)),
                          r=[QK.reg(2 * NTOK + j * 128, 2 * NTOK + (j + 1) * 128), QK.reg(i * 128, (i + 1) * 128),
                             QK.reg(NTOK + i * 128, NTOK + (i + 1) * 128)], w=[o_rg], sig=(mk is None and jj == nk - 1))
                    if mk is not None:
                        fw.op("pe", PE_.matmul, dict(out=o_ap, lhsT=IDB[:, :], rhs=MASKA[:, mk * 256:(mk + 1) * 256], start=False, stop=True),
                              r=[IDB.reg(), MASKA.reg()], w=[o_rg], sig=(jj == nk - 1))

            def EXP_part(u):
                i, g, unit = u
                nk = len(ktl_of(i))
                sb = (unit % 2) * 1536
                pt = PT[unit % 2]
                fw.op("act", S_.activation, dict(out=pt.ap()[:, 0:nk * 256], in_=PS[:, sb: sb + nk * 256], func=AF.Exp, scale=0.125),
                      r=[PS.reg(sb, sb + nk * 256)], w=[pt.reg(0, nk * 256)])

            def PV_part(u):
                i, g, unit = u
                ktl = ktl_of(i)
                nk = len(ktl)
                pt = PT[unit % 2]
                abank = 6 + (unit % 2)
                for r_ in range(2):
                    for jj, (j, mk) in enumerate(ktl):
                        fw.op("pe", PE_.matmul, dict(out=psb(abank, r_ * 128, r_ * 128 + 65), lhsT=pt.ap()[:, jj * 256 + r_ * 128: jj * 256 + (r_ + 1) * 128],
                                                     rhs=VB.ap()[:, j, g, 0:65], start=(jj == 0), stop=(jj == nk - 1)),
                              r=[pt.reg(jj * 256 + r_ * 128, jj * 256 + (r_ + 1) * 128), VB.reg(j * 260 + g * 65, j * 260 + (g + 1) * 65)],
                              w=[psr(abank, r_ * 128, r_ * 128 + 65)], sig=(jj == nk - 1))

            def POST_a(u):
                i, g, unit = u
                yt = YTILE[i % 2]
                abank = 6 + (unit % 2)
                acc = psb(abank, 0, 256).rearrange("p (r d) -> p r d", r=2)
                den = sm(TMPS + 60, TMPS + 62)
                fw.op("dve", V.tensor_tensor, dict(out=den, in0=acc[:, :, 64], in1=sm(ESINK + 4 * l + 2 * g, ESINK + 4 * l + 2 * g + 2), op=ALU.add),
                      r=[psr(abank, 0, 256), smr(ESINK, ESINK + 8)], w=[smr(TMPS + 60, TMPS + 62)])
                fw.op("dve", V.reciprocal, dict(out=den, in_=den), r=[smr(TMPS + 60, TMPS + 62)], w=[smr(TMPS + 60, TMPS + 62)])
                fw.op("dve", V.tensor_tensor, dict(out=yt.ap()[:, 0, 128 * g:128 * g + 128].rearrange("p (r d) -> p r d", r=2), in0=acc[:, :, 0:64],
                                                   in1=den.unsqueeze(2).broadcast_to([128, 2, 64]), op=ALU.mult),
                      r=[psr(abank, 0, 256), smr(TMPS + 60, TMPS + 62)], w=[yt.reg(128 * g, 128 * g + 128)])

            def POST_b(u):
                i, g, unit = u
                y_transposes(YTILE[i % 2], [(0, i * 128)], 6 + (i % 2), 512)

            S_part(units[0])
            deferred_b = None
            for k, u in enumerate(units):
                EXP_part(u)
                if k + 1 < len(units):
                    S_part(units[k + 1])
                PV_part(u)
                if deferred_b is not None:
                    POST_b(deferred_b)
                    deferred_b = None
                POST_a(u)
                if u[1] == 1:
                    deferred_b = u
            if deferred_b is not None:
                POST_b(deferred_b)
            outproj(l, [0, 1, 2, 3] + ([4] if ctx_needed else []))

        def mixer_B(l, ctx_needed):
            load_wm(l, 512, 512)
            load_wo(l, 1)
            HB = AV(AR, SP0, [2, 2364])
            DG = AV(AR, SP0 + 4736, [2, 31, 128])
            b1 = SP0 + 4736 + 7936
            CV = AV(AR, b1, [2, 512], F32)
            MEAN = AV(AR, b1 + 2048, [512], F32)
            RSD = AV(AR, b1 + 3072, [512], F32)
            SQc = [AV(AR, b1 + 4096 + 1024 * i, [512], F32) for i in range(2)]
            UU = [AV(AR, b1 + 6144 + 1024 * i, [512], F32) for i in range(2)]
            for (a_, b_) in ((0, 15), (2063, 2093), (2349, 2364)):
                fw.op("dve", V.memset, dict(ap=HB.ap()[:, :, a_:b_], constant=0.0), w=[HB.reg(a_, b_), HB.reg(2364 + a_, 2364 + b_)])
            cw0 = 164 + 62 * l
            for c in range(2):
                for k in range(31):
                    col = cw0 + 2 * k + c
                    fw.op("dve", V.tensor_scalar, dict(out=DG.ap()[:, c, k, :], in0=IDF[:, :], scalar1=VEC[:, col:col + 1], scalar2=None, op0=ALU.mult),
                          r=[IDF.reg(), VEC.reg()], w=[DG.reg((c * 31 + k) * 128, (c * 31 + k + 1) * 128)])

            def evac(t, banks):
                b = banks[0]
                qs = QKS[t % 2]
                fw.op("act", S_.activation, dict(out=RT[0].ap(), in_=psb(b, 256, 512), func=AF.Exp, scale=-1.0), r=[psr(b, 256, 512)], w=[RT[0].reg()])
                fw.op("dve", V.tensor_scalar, dict(out=RT[0].ap(), in0=RT[0].ap(), scalar1=1.0, scalar2=None, op0=ALU.add), r=[RT[0].reg()], w=[RT[0].reg()])
                fw.op("dve", V.reciprocal, dict(out=RT[0].ap(), in_=RT[0].ap()), r=[RT[0].reg()], w=[RT[0].reg()])
                fw.op("dve", V.tensor_tensor, dict(out=qs.ap()[:, 0:256], in0=psb(b, 0, 256), in1=RT[0].ap(), op=ALU.mult),
                      r=[psr(b, 0, 256), RT[0].reg()], w=[qs.reg(0, 256)])
                tok0 = 15 + t * 128 if t < 16 else 2093 + (t - 16) * 128
                return lambda: qk_transposes(t, qs, 2, HB, tok0)

            inproj(l, 512, evac)
            blocks = [0, 1, 2, 3] + ([4] if ctx_needed else [])
            for bi in blocks:
                t0, n = TB[bi]
                hb0 = t0 if bi < 4 else 2078
                for c in range(2):
                    bank = c
                    for k in range(31):
                        fw.op("pe", PE_.matmul, dict(out=psb(bank, 0, n), lhsT=DG.ap()[:, c, k, :], rhs=HB.ap()[:, c, hb0 + k: hb0 + k + n],
                                                     start=(k == 0), stop=(k == 30)),
                              r=[DG.reg((c * 31 + k) * 128, (c * 31 + k + 1) * 128), HB.reg(c * 2364 + hb0 + k, c * 2364 + hb0 + k + n)],
                              w=[psr(bank, 0, n)], sig=(k == 30))
                    cb = 152 + 2 * l + c
                    fw.op("act", S_.activation, dict(out=CV.ap()[:, c, 0:n], in_=psb(bank, 0, n), func=AF.Identity, bias=VEC[:, cb:cb + 1]),
                          r=[psr(bank, 0, n), VEC.reg()], w=[CV.reg(c * 512, c * 512 + n)])
                    fw.op("act", S_.activation, dict(out=SQc[c].ap()[:, 0:n], in_=CV.ap()[:, c, 0:n], func=AF.Square),
                          r=[CV.reg(c * 512, c * 512 + n)], w=[SQc[c].reg(0, n)])
                for c in range(2):
                    fw.op("pe", PE_.matmul, dict(out=psb(2, 0, n), lhsT=ONF[:, :], rhs=CV.ap()[:, c, 0:n], start=(c == 0), stop=(c == 1)),
                          r=[ONF.reg(), CV.reg(c * 512, c * 512 + n)], w=[psr(2, 0, n)], sig=(c == 1))
                for c in range(2):
                    fw.op("pe", PE_.matmul, dict(out=psb(3, 0, n), lhsT=ONF[:, :], rhs=SQc[c].ap()[:, 0:n], start=(c == 0), stop=(c == 1)),
                          r=[ONF.reg(), SQc[c].reg(0, n)], w=[psr(3, 0, n)], sig=(c == 1))
                fw.op("dve", V.tensor_scalar, dict(out=MEAN.ap()[:, 0:n], in0=psb(2, 0, n), scalar1=1.0 / 256, scalar2=None, op0=ALU.mult),
                      r=[psr(2, 0, n)], w=[MEAN.reg(0, n)])
                fw.op("dve", V.tensor_tensor, dict(out=RSD.ap()[:, 0:n], in0=MEAN.ap()[:, 0:n], in1=MEAN.ap()[:, 0:n], op=ALU.mult),
                      r=[MEAN.reg(0, n)], w=[RSD.reg(0, n)])
                fw.op("dve", V.scalar_tensor_tensor, dict(out=RSD.ap()[:, 0:n], in0=psb(3, 0, n), scalar=1.0 / 256, in1=RSD.ap()[:, 0:n],
                                                          op0=ALU.mult, op1=ALU.subtract),
                      r=[psr(3, 0, n), RSD.reg(0, n)], w=[RSD.reg(0, n)])
                fw.op("act", S_.activation, dict(out=RSD.ap()[:, 0:n], in_=RSD.ap()[:, 0:n], func=AF.Ln, bias=sm(EPSC, EPSC + 1)),
                      r=[RSD.reg(0, n), smr(EPSC, EPSC + 1)], w=[RSD.reg(0, n)])
                fw.op("act", S_.activation, dict(out=RSD.ap()[:, 0:n], in_=RSD.ap()[:, 0:n], func=AF.Exp, scale=-0.5), r=[RSD.reg(0, n)], w=[RSD.reg(0, n)])
                for c in range(2):
                    fw.op("dve", V.tensor_tensor, dict(out=UU[c].ap()[:, 0:n], in0=CV.ap()[:, c, 0:n], in1=MEAN.ap()[:, 0:n], op=ALU.subtract),
                          r=[CV.reg(c * 512, c * 512 + n), MEAN.reg(0, n)], w=[UU[c].reg(0, n)])
                    fw.op("dve", V.tensor_tensor, dict(out=UU[c].ap()[:, 0:n], in0=UU[c].ap()[:, 0:n], in1=RSD.ap()[:, 0:n], op=ALU.mult),
                          r=[UU[c].reg(0, n), RSD.reg(0, n)], w=[UU[c].reg(0, n)])
                    lg, lb = 156 + 2 * l + c, 160 + 2 * l + c
                    fw.op("act", S_.activation, dict(out=UU[c].ap()[:, 0:n], in_=UU[c].ap()[:, 0:n], func=AF.Identity,
                                                     scale=VEC[:, lg:lg + 1], bias=VEC[:, lb:lb + 1]),
                          r=[UU[c].reg(0, n), VEC.reg()], w=[UU[c].reg(0, n)])
                    fw.op("act", S_.activation, dict(out=SQc[c].ap()[:, 0:n], in_=UU[c].ap()[:, 0:n], func=AF.Exp, scale=-1.0),
                          r=[UU[c].reg(0, n)], w=[SQc[c].reg(0, n)])
                    fw.op("dve", V.tensor_scalar, dict(out=SQc[c].ap()[:, 0:n], in0=SQc[c].ap()[:, 0:n], scalar1=1.0, scalar2=None, op0=ALU.add),
                          r=[SQc[c].reg(0, n)], w=[SQc[c].reg(0, n)])
                    fw.op("dve", V.reciprocal, dict(out=SQc[c].ap()[:, 0:n], in_=SQc[c].ap()[:, 0:n]), r=[SQc[c].reg(0, n)], w=[SQc[c].reg(0, n)])
                    fw.op("dve", V.tensor_tensor, dict(out=YT.ap()[:, c, t0:t0 + n], in0=UU[c].ap()[:, 0:n], in1=SQc[c].ap()[:, 0:n], op=ALU.mult),
                          r=[UU[c].reg(0, n), SQc[c].reg(0, n)], w=[YT.reg(c * NTOK + t0, c * NTOK + t0 + n)])
            outproj(l, blocks)

        def mixer_C(l, ctx_needed):
            load_wm(l, 1024, 768)
            load_wo(l, 2)
            QK6 = AV(AR, SP0, [6, NTOK])
            VBc = AV(AR, SP0 + 13824, [NT, 4, 65])
            PTC0 = SP0 + 13824 + 4736
            vb_ones(VBc)

            def evac(t, banks):
                b0_, b1_ = banks
                qs = QKS[t % 2]
                if t < 16:
                    rope(t, psb(b0_, 0, 512), psr(b0_, 0, 512), 16, 16, 64, qs, 0)
                else:
                    fw.op("dve", V.tensor_copy, dict(out=qs.ap()[:, 0:512], in_=psb(b0_, 0, 512)), r=[psr(b0_, 0, 512)], w=[qs.reg(0, 512)])
                v_evac(t, psb(b1_, 0, 256), psr(b1_, 0, 256), 4, VBc)

                def part_b():
                    bank = 4 + (t % 2)
                    for c in range(4):
                        fw.op("pe", PE_.transpose, dict(out=psb16(bank, c * 128, (c + 1) * 128), in_=qs.ap()[:, c * 128:(c + 1) * 128], identity=IDB[:, :]),
                              r=[qs.reg(c * 128, (c + 1) * 128), IDB.reg()], w=[psr16(bank, c * 128, (c + 1) * 128)], sig=(c == 3))
                    tk = t * 128
                    pq = psb16(bank, 0, 256).rearrange("p (c t) -> p c t", c=2)
                    pk = psb16(bank, 256, 512).rearrange("p (c t) -> p c t", c=2)
                    fw.op("act", S_.activation, dict(out=QK6.ap()[:, 0:2, tk:tk + 128], in_=pq, func=AF.Identity, scale=VEC[:, 288:289]),
                          r=[psr16(bank, 0, 256), VEC.reg()], w=[QK6.reg(c * NTOK + tk, c * NTOK + tk + 128) for c in (0, 1)])
                    fw.op("act", S_.activation, dict(out=QK6.ap()[:, 2:4, tk:tk + 128], in_=pq, func=AF.Identity, scale=VEC[:, 289:290]),
                          r=[psr16(bank, 0, 256), VEC.reg()], w=[QK6.reg(c * NTOK + tk, c * NTOK + tk + 128) for c in (2, 3)])
                    fw.op("act", S_.copy, dict(out=QK6.ap()[:, 4:6, tk:tk + 128], in_=pk),
                          r=[psr16(bank, 256, 512)], w=[QK6.reg(c * NTOK + tk, c * NTOK + tk + 128) for c in (4, 5)])
                return part_b

            inproj(l, 768, evac)
            if OPTS["c_stage"] < 2:
                return
            PT = [AV(AR, PTC0 + 1024 * i, [1024]) for i in range(3)]
            scale = 32 ** -0.5
            qblocks = [(q0, 256, list(range(18))) for q0 in range(0, SEQ, 256)]
            if ctx_needed:
                qblocks.append((2048, 256, [16, 17]))
            sctr = 0
            last_off = [None]
            pending_tr = []
            pending_post = []
            for qbi, (q0, nq, ktl) in enumerate(qblocks):
                yt = YTILE[qbi % 2]
                for h in (0, 2, 1, 3):
                    nk = len(ktl)
                    off = 64 * (h % 2)
                    kc_ = 4 + h // 2

                    npair = nk // 2

                    def s_mm(p):
                        sb2 = (sctr + p) % 2
                        c0_ = h // 2
                        for e in range(2):
                            j = ktl[2 * p + e]
                            fw.op("pe", PE_.matmul, dict(out=psb(2 * sb2 + e, 0, 512).rearrange("p (m q) -> p m q", m=2),
                                                         lhsT=QK6.ap()[off:off + 64, kc_, j * 128:(j + 1) * 128],
                                                         rhs=QK6.ap()[off:off + 64, c0_:c0_ + 3:2, q0:q0 + nq], start=True, stop=True),
                                  r=[QK6.reg(kc_ * NTOK + j * 128, kc_ * NTOK + (j + 1) * 128), QK6.reg(c0_ * NTOK + q0, c0_ * NTOK + q0 + nq),
                                     QK6.reg((c0_ + 2) * NTOK + q0, (c0_ + 2) * NTOK + q0 + nq)],
                                  w=[psr(2 * sb2 + e, 0, 512)], sig=(e == 1))

                    if last_off[0] != off and fw.cnt["pe"] > 0:
                        PE_.wait_ge(fw.sem["pe"], fw.cnt["pe"])
                    last_off[0] = off
                    s_mm(0)
                    for p in range(npair):
                        sb2 = (sctr + p) % 2
                        pt = PT[(sctr + p) % 3]
                        fw.op("act", S_.activation, dict(out=pt.ap(), in_=PS[:, 2 * sb2 * 512:(2 * sb2 + 2) * 512], func=AF.Exp, scale=scale),
                              r=[psr(2 * sb2), psr(2 * sb2 + 1)], w=[pt.reg()])
                        if p + 1 < npair:
                            s_mm(p + 1)
                        for e in range(2):
                            j = ktl[2 * p + e]
                            for m in range(2):
                                for qt in range(2):
                                    ab = 4 + 2 * m + qt
                                    o_ = e * 512 + m * 256 + qt * 128
                                    fw.op("pe", PE_.matmul, dict(out=psb(ab, 0, 65), lhsT=pt.ap()[:, o_:o_ + 128],
                                                                 rhs=VBc.ap()[:, j, h, 0:65], start=(p == 0 and e == 0), stop=(p == npair - 1 and e == 1)),
                                          r=[pt.reg(o_, o_ + 128), VBc.reg(j * 260 + h * 65, j * 260 + (h + 1) * 65)],
                                          w=[psr(ab, 0, 65)], sig=((p == npair - 1 and e == 1) or (e == 1 and m == 1 and qt == 1)))
                        if p == 0 and pending_post:
                            pending_post.pop(0)()
                        if p == 1 and pending_tr:
                            pending_tr.pop(0)()
                    nk = npair
                    sctr += nk
                    if OPTS["c_stage"] < 5:
                        continue
                    accall = PS[:, 2048:4096].rearrange("p (b c) -> p b c", b=4)
                    accr = [psr(4, 0, 65), psr(5, 0, 65), psr(6, 0, 65), psr(7, 0, 65)]
                    rec = sm(TMPS + 64, TMPS + 68)
                    recr = smr(TMPS + 64, TMPS + 68)
                    fw.op("dve", V.reciprocal, dict(out=rec, in_=accall[:, :, 64]), r=accr, w=[recr])
                    fw.op("dve", V.tensor_scalar, dict(out=rec[:, 2:4], in0=rec[:, 2:4], scalar1=sm(NEGLAM + l, NEGLAM + l + 1), scalar2=None, op0=ALU.mult),
                          r=[recr, smr(NEGLAM, NEGLAM + 2)], w=[recr])
                    A_ = PPT.ap()[:, 0:128].rearrange("p (q d) -> p q d", q=2)
                    B_ = PPT.ap()[:, 256:384].rearrange("p (q d) -> p q d", q=2)
                    C_ = PPT.ap()[:, 512:640].rearrange("p (q d) -> p q d", q=2)
                    ar_, br_, cr_ = PPT.reg(0, 128), PPT.reg(256, 384), PPT.reg(512, 640)
                    fw.op("dve", V.tensor_tensor, dict(out=A_, in0=accall[:, 0:2, 0:64], in1=rec[:, 0:2].unsqueeze(2).broadcast_to([128, 2, 64]), op=ALU.mult),
                          r=accr[0:2] + [recr], w=[ar_])
                    fw.op("dve", V.tensor_tensor, dict(out=B_, in0=accall[:, 2:4, 0:64], in1=rec[:, 2:4].unsqueeze(2).broadcast_to([128, 2, 64]), op=ALU.mult),
                          r=accr[2:4] + [recr], w=[br_])
                    fw.op("dve", V.tensor_tensor, dict(out=A_, in0=A_, in1=B_, op=ALU.add), r=[ar_, br_], w=[ar_])
                    fw.op("dve", V.tensor_tensor, dict(out=C_, in0=A_, in1=A_, op=ALU.mult), r=[ar_], w=[cr_])
                    ss = sm(TMPS + 72, TMPS + 74)
                    ssr = smr(TMPS + 72, TMPS + 74)
                    fw.op("dve", V.tensor_reduce, dict(out=ss, in_=C_, axis=mybir.AxisListType.X, op=ALU.add), r=[cr_], w=[ssr])
                    def post_b(yt=yt, h=h, A_=A_, ss=ss, ssr=ssr, ar_=ar_):
                        fw.op("act", S_.activation, dict(out=ss, in_=ss, func=AF.Ln, bias=sm(EPSC, EPSC + 1), scale=1.0 / 64),
                              r=[ssr, smr(EPSC, EPSC + 1)], w=[ssr])
                        fw.op("act", S_.activation, dict(out=ss, in_=ss, func=AF.Exp, scale=-0.5), r=[ssr], w=[ssr])
                        fw.op("dve", V.tensor_tensor, dict(out=A_, in0=A_, in1=ss.unsqueeze(2).broadcast_to([128, 2, 64]), op=ALU.mult),
                              r=[ar_, ssr], w=[ar_])
                        fw.op("dve", V.tensor_tensor, dict(out=yt.ap()[:, 0:2, h * 64:(h + 1) * 64], in0=A_,
                                                           in1=sm(SGV + 64 * l, SGV + 64 * l + 64).unsqueeze(1).broadcast_to([128, 2, 64]), op=ALU.mult),
                              r=[ar_, smr(SGV, SGV + 128)], w=[yt.reg(qt * 256 + h * 64, qt * 256 + (h + 1) * 64) for qt in range(2)])
                    pending_post.append(post_b)
                if OPTS["c_stage"] < 6:
                    continue
                while pending_tr:
                    pending_tr.pop(0)()
                pending_tr.append(lambda yt=yt, q0=q0, qbi=qbi: y_transposes(yt, [(qt, q0 + qt * 128) for qt in range(2)], 3, 0))
            while pending_post:
                pending_post.pop(0)()
            while pending_tr:
                pending_tr.pop(0)()
            outproj(l, [0, 1, 2, 3] + ([4] if ctx_needed else []))

        def mixer_D(l, ctx_needed):
            load_wm(l, 1792, 768)
            load_wo(l, 3)
            vb_ones()
            NB = AV(AR, PT0 + 1792, [2, 21, 128])

            def evac(t, banks):
                b0_, b1_ = banks
                qs = QKS[t % 2]
                fw.op("act", S_.activation, dict(out=qs.ap()[:, 0:256], in_=psb(b0_, 0, 256), func=AF.Copy, scale=0.125),
                      r=[psr(b0_, 0, 256)], w=[qs.reg(0, 256)])
                fw.op("dve", V.tensor_copy, dict(out=qs.ap()[:, 256:512], in_=psb(b0_, 256, 512)), r=[psr(b0_, 256, 512)], w=[qs.reg(256, 512)])
                v_evac(t, psb(b1_, 0, 256), psr(b1_, 0, 256), 4)
                return lambda: qk_transposes(t, qs, 4, QK, t * 128)

            inproj(l, 768, evac)
            PT = [AV(AR, PT0 + 896 * i, [896]) for i in range(2)]
            qtiles = list(range(16)) + ([16, 17] if ctx_needed else [])
            hus = []
            unit = 0
            for hp in range(2):
                for i in qtiles:
                    for hh in range(2):
                        hus.append((hp, i, hh, unit))
                    unit += 1

            def ktl_of(i):
                if i < 16:
                    return [(j, na_block_index(i, j)) for j in na_ktiles(i)] + [(16, None), (17, None)]
                return [(16, None), (17, None)]

            def S_part(hu):
                hp, i, hh, unit = hu
                if i == qtiles[0] and hh == 0:
                    fw.dma("pool", NB.ap(), nab_d[l, hp].rearrange("p (h b q) -> p h b q", h=2, b=21), w=[NB.reg()])
                ktl = ktl_of(i)
                nk = len(ktl)
                sb = hh * 1024
                off = 64 * hh
                nloc = sum(1 for (_, bidx) in ktl if bidx is not None)
                for jj, (j, bidx) in enumerate(ktl):
                    o_ap = PS[:, sb + jj * 128: sb + (jj + 1) * 128]
                    o_rg = PS.reg(sb + jj * 128, sb + (jj + 1) * 128)
                    fw.op("pe", PE_.matmul, dict(out=o_ap, lhsT=QK.ap()[off:off + 64, 2 + hp, j * 128:(j + 1) * 128],
                                                 rhs=QK.ap()[off:off + 64, hp, i * 128:(i + 1) * 128], start=(jj % 4 == 0), stop=True,
                                                 skip_group_check=True),
                          r=[QK.reg((2 + hp) * NTOK + j * 128, (2 + hp) * NTOK + (j + 1) * 128), QK.reg(hp * NTOK + i * 128, hp * NTOK + (i + 1) * 128)],
                          w=[o_rg], sig=(nloc == 0 and jj == nk - 1))
                if nloc > 0:
                    b0 = ktl[0][1]
                    n1 = min(nloc, 4)
                    fw.op("pe", PE_.matmul, dict(out=PS[:, sb: sb + n1 * 128], lhsT=IDB[:, :],
                                                 rhs=NB.ap()[:, hh, b0:b0 + n1, :].rearrange("p b q -> p (b q)"), start=False, stop=True,
                                                 skip_group_check=True),
                          r=[IDB.reg(), NB.reg((hh * 21 + b0) * 128, (hh * 21 + b0 + n1) * 128)], w=[PS.reg(sb, sb + n1 * 128)], sig=(nloc <= 4))
                    if nloc > 4:
                        fw.op("pe", PE_.matmul, dict(out=PS[:, sb + 512: sb + 640], lhsT=IDB[:, :], rhs=NB.ap()[:, hh, b0 + 4, :], start=False, stop=True,
                                                     skip_group_check=True),
                              r=[IDB.reg(), NB.reg((hh * 21 + b0 + 4) * 128, (hh * 21 + b0 + 5) * 128)], w=[PS.reg(sb + 512, sb + 640)])

            def EXP_part(hu):
                hp, i, hh, unit = hu
                nk = len(ktl_of(i))
                sb = hh * 1024
                pt = PT[hh]
                fw.op("act", S_.activation, dict(out=pt.ap()[:, 0:nk * 128], in_=PS[:, sb: sb + nk * 128], func=AF.Exp),
                      r=[PS.reg(sb, sb + nk * 128)], w=[pt.reg(0, nk * 128)])

            def PV_part(hu):
                hp, i, hh, unit = hu
                ktl = ktl_of(i)
                nk = len(ktl)
                h = 2 * hp + hh
                abank = 4 + (unit % 2)
                pt = PT[hh]
                for jj, (j, bidx) in enumerate(ktl):
                    fw.op("pe", PE_.matmul, dict(out=psb(abank, hh * 128, hh * 128 + 65), lhsT=pt.ap()[:, jj * 128:(jj + 1) * 128],
                                                 rhs=VB.ap()[:, j, h, 0:65], start=(jj == 0), stop=(jj == nk - 1)),
                          r=[pt.reg(jj * 128, (jj + 1) * 128), VB.reg(j * 260 + h * 65, j * 260 + (h + 1) * 65)],
                          w=[psr(abank, hh * 128, hh * 128 + 65)], sig=(jj == nk - 1))

            def POST_a(hu):
                hp, i, hh, unit = hu
                yt = YTILE[i % 2]
                abank = 4 + (unit % 2)
                acc = psb(abank, 0, 256).rearrange("p (r d) -> p r d", r=2)
                den = sm(TMPS + 80, TMPS + 82)
                denr = smr(TMPS + 80, TMPS + 82)
                fw.op("dve", V.reciprocal, dict(out=den, in_=acc[:, :, 64]), r=[psr(abank, 0, 256)], w=[denr])
                fw.op("dve", V.tensor_tensor, dict(out=yt.ap()[:, hp, 0:128].rearrange("p (r d) -> p r d", r=2), in0=acc[:, :, 0:64],
                                                   in1=den.unsqueeze(2).broadcast_to([128, 2, 64]), op=ALU.mult),
                      r=[psr(abank, 0, 256), denr], w=[yt.reg(hp * 256, hp * 256 + 128)])

            def POST_b(hu):
                hp, i, hh, unit = hu
                yt = YTILE[i % 2]
                tb_ = 6 + (unit % 2)
                fw.op("pe", PE_.transpose, dict(out=psb16(tb_, 0, 128), in_=yt.ap()[:, hp, 0:128], identity=IDB[:, :]),
                      r=[yt.reg(hp * 256, hp * 256 + 128), IDB.reg()], w=[psr16(tb_, 0, 128)])
                fw.op("act", S_.copy, dict(out=YT.ap()[:, hp, i * 128:(i + 1) * 128], in_=psb16(tb_, 0, 128)),
                      r=[psr16(tb_, 0, 128)], w=[YT.reg(hp * NTOK + i * 128, hp * NTOK + (i + 1) * 128)])

            S_part(hus[0])
            deferred_b = None
            for k, hu in enumerate(hus):
                EXP_part(hu)
                if k + 1 < len(hus):
                    S_part(hus[k + 1])
                PV_part(hu)
                if deferred_b is not None:
                    POST_b(deferred_b)
                    deferred_b = None
                if hu[2] == 1:
                    POST_a(hu)
                    deferred_b = hu
            if deferred_b is not None:
                POST_b(deferred_b)
            outproj(l, [0, 1, 2, 3] + ([4] if ctx_needed else []))

        PASSES = [(0, 4), (4, 4), (8, 4), (12, 4), (16, 3), (19, 3)]

        def ffn(l, ctx_needed, prefetch_ada=None):
            WG = [AV(AR, 4096 * i, [8, 512]) for i in range(2)]
            WU = [AV(AR, 8192 + 4096 * i, [8, 512]) for i in range(2)]
            WD = [AV(AR, 16384 + 4096 * i, [4, 1024]) for i in range(2)]
            AT = [AV(AR, 24576 + 2048 * i, [4, 512]) for i in range(2)]
            SG = [AV(AR, 28672 + 1024 * i, [512], F32) for i in range(2)]
            gv = w_gate_d[l].rearrange("(k p) n -> p k n", p=128)
            uv = w_up_d[l].rearrange("(k p) n -> p k n", p=128)
            blocks = [0, 1, 2, 3] + ([4] if ctx_needed else [])
            cnt = 0
            ada_q = []
            if prefetch_ada is not None:
                dmas, mms, fin = ada_steps(prefetch_ada, base=30720, bank=7)
                ada_q = [dmas[0]] + [(lambda g=g: (dmas[g + 1]() if g + 1 < 12 else None, mms[g]())) for g in range(12)] + [fin]

            def load_pass(pi):
                m0, nm = PASSES[pi]
                wg, wu, wd = WG[pi % 2], WU[pi % 2], WD[pi % 2]
                fw.dma("pool", wg.ap()[:, :, 0:nm * 128], gv[:, :, m0 * 128:(m0 + nm) * 128], w=[wg.reg()])
                fw.dma("pool", wu.ap()[:, :, 0:nm * 128], uv[:, :, m0 * 128:(m0 + nm) * 128], w=[wu.reg()])
                fw.dma("pool", wd.ap()[:, 0:nm, :], w_down_d[l][m0 * 128:(m0 + nm) * 128, :].rearrange("(m p) n -> p m n", p=128), w=[wd.reg()])

            load_pass(0)
            for pi, (m0, nm) in enumerate(PASSES):
                wg, wu, wd = WG[pi % 2], WU[pi % 2], WD[pi % 2]
                if pi + 1 < len(PASSES):
                    load_pass(pi + 1)
                for bi in blocks:
                    t0, n = TB[bi]
                    s = 1 if bi == 4 else 0
                    at = AT[cnt % 2]
                    for mm in range(nm):
                        gb, ub = 2 * (mm % 2), 2 * (mm % 2) + 1
                        for (bank, wt) in ((gb, wg), (ub, wu)):
                            for kc in range(8):
                                fw.op("pe", PE_.matmul, dict(out=psb(bank, 0, n), lhsT=wt.ap()[:, kc, mm * 128:(mm + 1) * 128], rhs=HT[:, kc, t0:t0 + n],
                                                             start=(kc == 0), stop=(kc == 7)),
                                      r=[wt.reg(kc * 512 + mm * 128, kc * 512 + (mm + 1) * 128), ht_reg(kc, t0, n)], w=[psr(bank, 0, n)], sig=(kc == 7))
                        sg = SG[mm % 2]
                        fw.op("act", S_.activation, dict(out=sg.ap()[:, 0:n], in_=psb(gb, 0, n), func=AF.Silu), r=[psr(gb, 0, n)], w=[sg.reg(0, n)])
                        fw.op("dve", V.tensor_tensor, dict(out=at.ap()[:, mm, 0:n], in0=psb(ub, 0, n), in1=sg.ap()[:, 0:n], op=ALU.mult),
                              r=[psr(ub, 0, n), sg.reg(0, n)], w=[at.reg(mm * 512, mm * 512 + n)])
                    for oc in range(8):
                        bank = 4 + (oc % 3)
                        for mm in range(nm):
                            fw.op("pe", PE_.matmul, dict(out=psb(bank, 0, n), lhsT=wd.ap()[:, mm, oc * 128:(oc + 1) * 128], rhs=at.ap()[:, mm, 0:n],
                                                         start=(mm == 0), stop=(mm == nm - 1)),
                                  r=[wd.reg(mm * 1024 + oc * 128, mm * 1024 + (oc + 1) * 128), at.reg(mm * 512, mm * 512 + n)], w=[psr(bank, 0, n)],
                                  sig=(mm == nm - 1))
                        fw.op("dve", V.scalar_tensor_tensor, dict(out=XT[:, oc, t0:t0 + n], in0=psb(bank, 0, n), scalar=MOD[:, l, 40 + oc, s:s + 1],
                                                                  in1=XT[:, oc, t0:t0 + n], op0=ALU.mult, op1=ALU.add),
                              r=[psr(bank, 0, n), MOD.reg(l * 96, (l + 1) * 96), xt_reg(oc, t0, n)], w=[xt_reg(oc, t0, n)])
                    cnt += 1
                    if ada_q:
                        ada_q.pop(0)()
            while ada_q:
                ada_q.pop(0)()

        def load_x():
            for t in range(NT):
                stg = AV(AR, 8192 + 2048 * (t % 2), [1024], F32)
                src = x_d[t * 128:(t + 1) * 128, :] if t < 16 else ctx_d[(t - 16) * 128:(t - 15) * 128, :]
                fw.dma("sp", stg.ap(), src, w=[stg.reg()])
                for hb in range(2):
                    bank = (t % 2) * 2 + hb
                    for cc in range(4):
                        c = hb * 4 + cc
                        fw.op("pe", PE_.transpose, dict(out=psb(bank, cc * 128, (cc + 1) * 128), in_=stg.ap()[:, c * 128:(c + 1) * 128],
                                                        identity=IDF[:, :]),
                              r=[stg.reg(c * 128, (c + 1) * 128), IDF.reg()], w=[psr(bank, cc * 128, (cc + 1) * 128)], sig=(cc == 3))
                    eng = "act" if hb == 0 else "dve"
                    fn = S_.copy if hb == 0 else V.tensor_copy
                    fw.op(eng, fn, dict(out=XT[:, hb * 4:hb * 4 + 4, t * 128:(t + 1) * 128],
                                        in_=psb(bank).rearrange("p (c t) -> p c t", c=4)),
                          r=[psr(bank)], w=[xt_reg(c_, t * 128, 128) for c_ in range(hb * 4, hb * 4 + 4)])


        mix_fns = {"A": mixer_A, "B": mixer_B, "C": mixer_C, "D": mixer_D}
        ada(layers[0])
        load_x()
        for l in layers:
            ctx_needed = l < NL - 1
            if l != layers[0] and not do_ffn:
                ada(l)
            norm(lambda c, s, l=l: GP[:, l, 0, c, s:s + 1], lambda c, s, l=l: MOD[:, l, c, s:s + 1], [0, 1, 2, 3, 4], ht_out,
                 extra_r=[GP.reg(l * 32, l * 32 + 32), MOD.reg(l * 96, (l + 1) * 96)])
            for m in mixers:
                mix_fns[m](l, ctx_needed)
            if do_ffn:
                norm(lambda c, s, l=l: GP[:, l, 1, c, s:s + 1], lambda c, s, l=l: MOD[:, l, 24 + c, s:s + 1],
                     [0, 1, 2, 3] + ([4] if ctx_needed else []), ht_out,
                     extra_r=[GP.reg(l * 32, l * 32 + 32), MOD.reg(l * 96, (l + 1) * 96)])
                ffn(l, ctx_needed, prefetch_ada=(l + 1 if (l + 1) in layers else None))

        out_events = []
        if do_final:
            FNs = [AV(AR, 8192 * i, [8, 512], F32) for i in range(2)]
            OS = [AV(AR, 16384 + 2048 * i, [1024], F32) for i in range(2)]

            def fn_out(c, bi):
                FN = FNs[bi % 2]
                return FN.ap()[:, c, :], [FN.reg(c * 512, (c + 1) * 512)]

            def emit_out(bi):
                FN = FNs[bi % 2]
                for tt in range(4):
                    t = bi * 4 + tt
                    os_ = OS[t % 2]
                    for hb in range(2):
                        bank = (t % 2) * 2 + hb
                        for cc in range(4):
                            c = hb * 4 + cc
                            fw.op("pe", PE_.transpose, dict(out=psb(bank, cc * 128, (cc + 1) * 128), in_=FN.ap()[:, c, tt * 128:(tt + 1) * 128], identity=IDF[:, :]),
                                  r=[FN.reg(c * 512 + tt * 128, c * 512 + (tt + 1) * 128), IDF.reg()], w=[psr(bank, cc * 128, (cc + 1) * 128)], sig=(cc == 3))
                        if hb == 0:
                            fw.op("act", S_.copy, dict(out=os_.ap()[:, 0:512], in_=psb(bank)), r=[psr(bank)], w=[os_.reg(0, 512)])
                        else:
                            fw.op("dve", V.tensor_copy, dict(out=os_.ap()[:, 512:1024], in_=psb(bank)), r=[psr(bank)], w=[os_.reg(512, 1024)])
                    out_events.append(fw.dma("sp", out_d[t * 128:(t + 1) * 128, :], os_.ap(), r=[os_.reg()]))

            norm(lambda c, s: VEC[:, 144 + c:145 + c], None, [0, 1, 2, 3], fn_out, after_b=emit_out)
        if dbg:
            for c in range(8):
                out_events.append(fw.dma("sp", dbg_d[:, c * NTOK:(c + 1) * NTOK], XT[:, c, :], r=[xt_reg(c, 0, NTOK)]))
        for ev in out_events:
            fw.wait_event("sp", ev)
        build.stats = dict(nops=dict(fw.nops), nwaits=fw.nwaits, cnt=dict(fw.cnt))
    return nc


def _rope_tables():
    t = np.arange(SEQ)
    row = (t // 64).astype(np.float32)
    col = (t % 64).astype(np.float32)

    def tab(dim):
        nf = dim // 4
        inv = (10000.0 ** (-np.arange(nf, dtype=np.float32) / nf)).astype(np.float32)
        ang = np.concatenate([row[:, None] * inv, col[:, None] * inv], axis=-1).astype(np.float32)
        return np.cos(ang).astype(np.float32), np.sin(ang).astype(np.float32)

    ca, sa = tab(64)
    cc, sc = tab(32)
    return np.ascontiguousarray(np.concatenate([ca, sa, cc, sc], axis=1), dtype=np.float32)


def _mask_a():
    k = np.arange(128)[:, None]
    q = np.arange(128)[None, :]
    prev = np.where(q <= k, 0.0, NEG).astype(np.float32)
    nxt = np.where(k <= q, 0.0, NEG).astype(np.float32)
    return np.ascontiguousarray(np.concatenate([prev, prev, nxt, nxt], axis=1), dtype=np.float32)


def _na_bias(rpb):
    out = np.full((4, 21, 128, 128), NEG, dtype=np.float32)
    kk = np.arange(128)
    kr_l, kc = kk // 64, kk % 64
    qq = np.arange(128)
    qr_l, qc = qq // 64, qq % 64
    cs = np.clip(qc - 8, 0, 64 - 16)
    col_valid = (kc[:, None] >= cs[None, :]) & (kc[:, None] < cs[None, :] + 16)
    dc = np.clip(kc[:, None] - qc[None, :], -15, 15) + 15

    def block(i, j):
        krow = 2 * j + kr_l
        qrow = 2 * i + qr_l
        rs = np.clip(qrow - 4, 0, 24)
        row_valid = (krow[:, None] >= rs[None, :]) & (krow[:, None] < rs[None, :] + 8)
        dr = np.clip(krow[:, None] - qrow[None, :] + 7, 0, 14)
        valid = row_valid & col_valid
        return valid, dr

    for i in list(NA_EDGE) + [5]:
        for j in na_ktiles(i):
            bidx = na_block_index(i, j)
            valid, dr = block(i, j)
            for h in range(4):
                g = rpb[h][dr, dc]
                out[h, bidx] = np.where(valid, g, np.float32(NEG))
    o = out.reshape(2, 2, 21, 128, 128).transpose(0, 3, 1, 2, 4).reshape(2, 128, 2 * 21 * 128)
    return np.ascontiguousarray(o, dtype=np.float32)


_NC_CACHE = {}


def _prep_shared(inp):
    f = lambda a: np.ascontiguousarray(np.asarray(a, dtype=np.float32))
    w_in = f(inp["w_in"]).copy()
    qa = w_in[:, :, 0:256].reshape(NL, D, 4, 64)
    w_in[:, :, 0:256] = qa[:, :, [0, 2, 1, 3], :].reshape(NL, D, 256)
    bvec = np.zeros((NL, 196), np.float32)
    for l in range(NL):
        bvec[l, 0:4] = f(inp["attn_sink"])[l]
        bvec[l, 4:36] = f(inp["diff_lq1"])[l]
        bvec[l, 36:68] = f(inp["diff_lk1"])[l]
        bvec[l, 68:100] = f(inp["diff_lq2"])[l]
        bvec[l, 100:132] = f(inp["diff_lk2"])[l]
        bvec[l, 132:196] = f(inp["diff_subln_g"])[l]
    nab = np.stack([_na_bias(f(inp["na_rpb"])[l]) for l in range(NL)], axis=0)
    shared = dict(bvec=bvec.reshape(-1), rope=_rope_tables(), maska=_mask_a(), nab=np.ascontiguousarray(nab),
                  w_ada=f(inp["w_ada"]), w_in=w_in, w_out=f(inp["w_out"]), w_gate=f(inp["w_gate"]), w_up=f(inp["w_up"]),
                  w_down=f(inp["w_down"]))
    rows = np.zeros((384, 128), np.float32)
    rows[8:16] = f(inp["c_ctx"]).reshape(8, 128)
    for l in range(NL):
        rows[16 + 8 * l:24 + 8 * l] = f(inp["norm1_g"])[l].reshape(8, 128)
        rows[32 + 8 * l:40 + 8 * l] = f(inp["norm2_g"])[l].reshape(8, 128)
        rows[48 + 48 * l:96 + 48 * l] = f(inp["b_ada"])[l].reshape(48, 128)
        rows[152 + 2 * l:154 + 2 * l] = f(inp["conv_b"])[l].reshape(2, 128)
        rows[156 + 2 * l:158 + 2 * l] = f(inp["conv_ln_g"])[l].reshape(2, 128)
        rows[160 + 2 * l:162 + 2 * l] = f(inp["conv_ln_b"])[l].reshape(2, 128)
        rows[164 + 62 * l:226 + 62 * l] = f(inp["conv_w"])[l].reshape(31, 256).reshape(62, 128)
    rows[144:152] = f(inp["final_g"]).reshape(8, 128)
    pp = np.arange(128)
    rows[288] = ((pp % 64) < 32).astype(np.float32)
    rows[289] = ((pp % 64) >= 32).astype(np.float32)
    return shared, rows


def make_in_maps(inp, cores):
    shared, rows = _prep_shared(inp)
    x = np.asarray(inp["x"], dtype=np.float32)
    ctx = np.asarray(inp["ctx"], dtype=np.float32)
    c = np.asarray(inp["c"], dtype=np.float32)
    maps = []
    for b in cores:
        r = rows.copy()
        r[0:8] = c[b].reshape(8, 128)
        m = dict(shared)
        m.update(x=np.ascontiguousarray(x[b]), ctx=np.ascontiguousarray(ctx[b]), vecs=r)
        maps.append(m)
    return maps


def kernel(**inputs):
    if "nc" not in _NC_CACHE:
        _NC_CACHE["nc"] = build()
    nc = _NC_CACHE["nc"]
    in_maps = make_in_maps(inputs, list(range(8)))
    res = run_bass_kernel_spmd(nc, in_maps, core_ids=list(range(8)))
    out = np.stack([np.asarray(r["out"], dtype=np.float32) for r in res.results], axis=0)
    return out
```
